# Optimizing a Trainium2 kernel written in Bass

```python
import math
import jax, jax.numpy as jnp
from jax import lax
import numpy as np

D_MODEL = 1024
BATCH = 16
SEQ = 2048
DEPTH = 1

D_MIX = 2 * D_MODEL
M_WIDTH = D_MIX // 2
M_HEADS = 4
M_HD = M_WIDTH // M_HEADS
M_CHUNK = 64
CONV_K = 4
A_WIDTH = D_MIX - M_WIDTH
A_HEADS = 8
A_VHD = A_WIDTH // A_HEADS
A_QKD = A_VHD // 2
Q_BLOCK = 128
N_BUCKETS = 32
MAX_DIST = 128
EPS = 1e-6

SPLIT_SIZES = (M_WIDTH, M_WIDTH, M_WIDTH, M_WIDTH, M_WIDTH, M_HEADS, M_HEADS,
               A_WIDTH, A_WIDTH, A_WIDTH, A_WIDTH)
D_IN = 5 * M_WIDTH + 2 * M_HEADS + 4 * A_WIDTH

kernel_name = "hymba_mlstm_diffattn_hybrid"


def rmsnorm(x, w):
    xf = x.astype(jnp.float32)
    y = xf * lax.rsqrt(jnp.mean(xf * xf, axis=-1, keepdims=True) + EPS)
    return (y * w.astype(jnp.float32)).astype(x.dtype)


def head_layernorm(h, w):
    mu = jnp.mean(h, axis=-1, keepdims=True)
    var = jnp.mean(jnp.square(h - mu), axis=-1, keepdims=True)
    return (h - mu) * lax.rsqrt(var + EPS) * w.astype(jnp.float32)


def causal_dwconv(x, w):
    C = x.shape[-1]
    return lax.conv_general_dilated(
        x, w[:, None, :].astype(x.dtype), window_strides=(1,),
        padding=[(CONV_K - 1, 0)], dimension_numbers=('NWC', 'WIO', 'NWC'),
        feature_group_count=C)


def t5_bucket(q_pos, k_pos):
    n = jnp.maximum(q_pos[:, None] - k_pos[None, :], 0)
    max_exact = N_BUCKETS // 2
    nf = jnp.maximum(n, 1).astype(jnp.float32)
    large = max_exact + (jnp.log(nf / max_exact) / math.log(MAX_DIST / max_exact)
                         * (N_BUCKETS - max_exact)).astype(jnp.int32)
    large = jnp.minimum(large, N_BUCKETS - 1)
    return jnp.where(n < max_exact, n, large)


def mlstm_chunkwise(q, k, v, log_i, log_f):
    Bn, S, H, DH = q.shape
    NC = S // M_CHUNK

    def to_chunks(a):
        a = a.reshape((Bn, NC, M_CHUNK) + a.shape[2:])
        a = jnp.moveaxis(a, 3, 2)
        return jnp.moveaxis(a, 1, 0)

    qc = to_chunks(q) * (DH ** -0.5)
    kc = to_chunks(k)
    vc = to_chunks(v)
    lic = to_chunks(log_i)
    bc = jnp.cumsum(to_chunks(log_f), axis=-1)
    mask = jnp.tril(jnp.ones((M_CHUNK, M_CHUNK), dtype=bool))

    def step(carry, inp):
        C, n, m = carry
        qb, kb, vb, lib, bb = inp
        D = bb[..., :, None] - bb[..., None, :] + lib[..., None, :]
        D = jnp.where(mask, D, -jnp.inf)
        inter = bb + m[..., None]
        m_t = jnp.maximum(jnp.max(D, axis=-1), inter)
        Dw = jnp.exp(D - m_t[..., None])
        inter_w = jnp.exp(inter - m_t)
        s_qk = jnp.einsum('bhtd,bhsd->bhts', qb, kb) * Dw
        num = (inter_w[..., None] * jnp.einsum('bhed,bhtd->bhte', C, qb)
               + jnp.einsum('bhts,bhse->bhte', s_qk, vb))
        den = inter_w * jnp.einsum('bhd,bhtd->bht', n, qb) + jnp.sum(s_qk, axis=-1)
        h = num / jnp.maximum(jnp.abs(den), jnp.exp(-m_t))[..., None]
        bL = bb[..., -1]
        g = bL[..., None] - bb + lib
        m_new = jnp.maximum(bL + m, jnp.max(g, axis=-1))
        decay = jnp.exp(bL + m - m_new)
        w = jnp.exp(g - m_new[..., None])
        C_new = decay[..., None, None] * C + jnp.einsum('bhs,bhse,bhsd->bhed', w, vb, kb)
        n_new = decay[..., None] * n + jnp.einsum('bhs,bhsd->bhd', w, kb)
        return (C_new, n_new, m_new), h

    init = (jnp.zeros((Bn, H, DH, DH), jnp.float32),
            jnp.zeros((Bn, H, DH), jnp.float32),
            jnp.zeros((Bn, H), jnp.float32))
    _, hs = lax.scan(step, init, (qc, kc, vc, lic, bc))
    hs = jnp.transpose(hs, (1, 0, 3, 2, 4))
    return hs.reshape(Bn, S, H, DH)


def diff_attention(q, k, v, lam, rel_bias):
    S = q.shape[1]
    q = q * (A_QKD ** -0.5)
    outs = []
    for blk in range(S // Q_BLOCK):
        q0 = blk * Q_BLOCK
        ke = q0 + Q_BLOCK
        qb, kb, vb = q[:, q0:ke], k[:, :ke], v[:, :ke]
        logits = jnp.einsum('bqhmd,bkhmd->bmhqk', qb, kb).astype(jnp.float32)
        q_pos = jnp.arange(q0, ke)
        k_pos = jnp.arange(ke)
        bias = jnp.transpose(rel_bias[t5_bucket(q_pos, k_pos)], (2, 0, 1)).astype(jnp.float32)
        causal = k_pos[None, :] <= q_pos[:, None]
        logits = jnp.where(causal, logits + bias, -jnp.inf)
        p = jax.nn.softmax(logits, axis=-1)
        w = p[:, 0] - lam * p[:, 1]
        outs.append(jnp.einsum('bhqk,bkhe->bqhe', w, vb.astype(jnp.float32)))
    return jnp.concatenate(outs, axis=1)


def hybrid_layer(x, c, layer_idx, norm_w, w_ada, b_ada, w_in, b_i, b_f, conv_q_w,
                 conv_k_w, m_norm_w, lq1, lk1, lq2, lk2, a_norm_w, rel_bias, w_out):
    Bn, S, _ = x.shape
    mod = jax.nn.silu(c) @ w_ada + b_ada
    shift, scale, gate = jnp.split(mod, 3, axis=-1)
    h = rmsnorm(x, norm_w) * (1 + scale[:, None, :]) + shift[:, None, :]

    proj = h @ w_in
    split_pts = np.cumsum(SPLIT_SIZES)[:-1].tolist()
    mq, mk, mv, mo, mz, mi, mf, aq, ak, av, az = jnp.split(proj, split_pts, axis=-1)

    mq = jax.nn.silu(causal_dwconv(mq, conv_q_w))
    mk = jax.nn.silu(causal_dwconv(mk, conv_k_w))
    hs4 = (Bn, S, M_HEADS, M_HD)
    log_i = (mi + b_i).astype(jnp.float32)
    log_f = jax.nn.log_sigmoid((mf + b_f).astype(jnp.float32))
    hm = mlstm_chunkwise(mq.astype(jnp.float32).reshape(hs4),
                         mk.astype(jnp.float32).reshape(hs4),
                         mv.astype(jnp.float32).reshape(hs4), log_i, log_f)
    hm = head_layernorm(hm, m_norm_w.reshape(M_HEADS, M_HD))
    hm = jax.nn.sigmoid(mo.astype(jnp.float32)).reshape(hs4) * hm
    hm = hm.reshape(Bn, S, M_WIDTH).astype(x.dtype) * jax.nn.silu(mz)

    lam_init = 0.8 - 0.6 * math.exp(-0.3 * layer_idx)
    lam = (jnp.exp(jnp.sum(lq1.astype(jnp.float32) * lk1.astype(jnp.float32)))
           - jnp.exp(jnp.sum(lq2.astype(jnp.float32) * lk2.astype(jnp.float32))) + lam_init)
    ha = diff_attention(aq.reshape(Bn, S, A_HEADS, 2, A_QKD),
                        ak.reshape(Bn, S, A_HEADS, 2, A_QKD),
                        av.reshape(Bn, S, A_HEADS, A_VHD), lam, rel_bias)
    ha = rmsnorm(ha, a_norm_w) * (1 - lam_init)
    ha = ha.reshape(Bn, S, A_WIDTH).astype(x.dtype) * jax.nn.silu(az)

    y = jnp.concatenate([hm, ha], axis=-1) @ w_out
    return x + gate[:, None, :] * y


def setup_inputs(seed: int = 0) -> dict:
    key = jax.random.key(seed)
    ks = jax.random.split(key, 20)
    f32 = jnp.float32
    nrm = lambda k, s: jax.random.normal(k, s, f32)
    return {
        "x": nrm(ks[0], (BATCH, SEQ, D_MODEL)),
        "c": nrm(ks[1], (BATCH, D_MODEL)),
        "norm_w": 1.0 + 0.02 * nrm(ks[2], (DEPTH, D_MODEL)),
        "w_ada": 0.3 * D_MODEL ** -0.5 * nrm(ks[3], (DEPTH, D_MODEL, 3 * D_MODEL)),
        "b_ada": 0.01 * nrm(ks[4], (DEPTH, 3 * D_MODEL)),
        "w_in": D_MODEL ** -0.5 * nrm(ks[5], (DEPTH, D_MODEL, D_IN)),
        "b_i": 0.1 * nrm(ks[6], (DEPTH, M_HEADS)),
        "b_f": jnp.linspace(3.0, 6.0, M_HEADS, dtype=f32)[None, :] + 0.1 * nrm(ks[7], (DEPTH, M_HEADS)),
        "conv_q_w": CONV_K ** -0.5 * nrm(ks[8], (DEPTH, CONV_K, M_WIDTH)),
        "conv_k_w": CONV_K ** -0.5 * nrm(ks[9], (DEPTH, CONV_K, M_WIDTH)),
        "m_norm_w": 1.0 + 0.02 * nrm(ks[10], (DEPTH, M_WIDTH)),
        "lambda_q1": 0.1 * nrm(ks[11], (DEPTH, A_QKD)),
        "lambda_k1": 0.1 * nrm(ks[12], (DEPTH, A_QKD)),
        "lambda_q2": 0.1 * nrm(ks[13], (DEPTH, A_QKD)),
        "lambda_k2": 0.1 * nrm(ks[14], (DEPTH, A_QKD)),
        "a_norm_w": 1.0 + 0.02 * nrm(ks[15], (DEPTH, A_VHD)),
        "rel_bias": 0.5 * nrm(ks[16], (N_BUCKETS, A_HEADS)),
        "w_out": D_MIX ** -0.5 * nrm(ks[17], (DEPTH, D_MIX, D_MODEL)),
        "final_norm_w": 1.0 + 0.02 * nrm(ks[18], (D_MODEL,)),
    }


def reference(x, c, norm_w, w_ada, b_ada, w_in, b_i, b_f, conv_q_w, conv_k_w,
              m_norm_w, lambda_q1, lambda_k1, lambda_q2, lambda_k2, a_norm_w,
              rel_bias, w_out, final_norm_w):
    for l in range(DEPTH):
        x = hybrid_layer(x, c, l, norm_w[l], w_ada[l], b_ada[l], w_in[l], b_i[l], b_f[l],
                         conv_q_w[l], conv_k_w[l], m_norm_w[l], lambda_q1[l], lambda_k1[l],
                         lambda_q2[l], lambda_k2[l], a_norm_w[l], rel_bias, w_out[l])
    return rmsnorm(x, final_norm_w)
```

```python
import math
from contextlib import ExitStack

import numpy as np
import concourse.bass as bass
import concourse.mybir as mybir
from concourse.bass_utils import run_bass_kernel_spmd

F32 = mybir.dt.float32
BF16 = mybir.dt.bfloat16
ALU = mybir.AluOpType
AF = mybir.ActivationFunctionType

N_CORES = 8
D = 1024
S_LEN = 2048
NB = 2
NT = S_LEN // 128
KC = D // 128
MQ, MK, MV, MO, MZ, MI, AQ, AK, AV, AZ = 0, 1024, 2048, 3072, 4096, 5120, 5128, 6152, 7176, 8200
D_IN = 9224
EPS = 1e-6
WA = 260
WV = 132
NEG = -30000.0

COMPUTE = ("pe", "act", "dve", "pool")
ALLENG = COMPUTE + ("sp",)


class Op:
    __slots__ = ("eng", "fn", "deps", "signal", "seq", "dma", "sem", "val", "pv")

    def __init__(self, eng, fn, deps, dma):
        self.eng = eng
        self.fn = fn
        self.deps = deps
        self.signal = False
        self.seq = 0
        self.dma = dma
        self.sem = None
        self.val = 0
        self.pv = 0


class Sched:
    def __init__(self, nc, n_dma_sems=32):
        self.nc = nc
        self.ops = []
        self.tw = {}
        self.tr = {}
        self.eseq = {e: 0 for e in ALLENG}
        self.n_dma_sems = n_dma_sems

    SUBS = {}
    PH = "PHASE"

    def _expand(self, toks, ph):
        out = []
        for t in toks:
            out.append(t)
            if isinstance(t, tuple) and len(t) == 2 and t[0] == "ps" and t[1] in self.SUBS:
                out.extend(self.SUBS[t[1]])
        if ph and self.PH not in out:
            out.append(self.PH)
        return out

    cut = None
    marks = {}

    def mark(self, name):
        if name not in self.marks:
            self.marks[name] = len(self.ops)

    def op(self, eng, fn, reads=(), writes=(), dma=False, ph=True):
        idx = len(self.ops)
        if self.cut is not None and idx >= self.cut:
            return -1
        deps = set()
        tw, tr = self.tw, self.tr
        writes = self._expand(writes, False)
        reads = self._expand(reads, ph and (self.PH not in writes))
        for t in reads:
            w = tw.get(t)
            if w is not None:
                deps.add(w)
        for t in writes:
            w = tw.get(t)
            if w is not None:
                deps.add(w)
            r = tr.get(t)
            if r:
                deps.update(r)
        for t in reads:
            tr.setdefault(t, []).append(idx)
        for t in writes:
            tw[t] = idx
            tr[t] = []
        o = Op(eng, fn, deps, dma)
        self.eseq[eng] += 1
        o.seq = self.eseq[eng]
        self.ops.append(o)
        return idx

    def dma(self, out, in_, reads=(), writes=(), q="sp", ph=True):
        return self.op(q, lambda e, o=out, i=in_: e.dma_start(out=o, in_=i), reads, writes, dma=True, ph=ph)

    def emit(self, stack):
        nc = self.nc
        ops = self.ops
        for o in ops:
            keep = set()
            best = {}
            for d in o.deps:
                p = ops[d]
                if p.dma:
                    keep.add(d)
                    continue
                if p.eng == o.eng:
                    if o.eng == "pe" and not o.dma:
                        continue
                    if o.seq - p.seq > 2:
                        continue
                if p.eng not in best or best[p.eng][0] < p.seq:
                    best[p.eng] = (p.seq, d)
            for _e, (_sq, d) in best.items():
                ops[d].signal = True
                keep.add(d)
            o.deps = keep
        sems = {e: stack.enter_context(nc.semaphore("s_" + e)) for e in COMPUTE}
        dpools = {"sp": [stack.enter_context(nc.semaphore("dh%d" % i)) for i in range(self.n_dma_sems)],
                  "pool": [stack.enter_context(nc.semaphore("ds%d" % i)) for i in range(8)]}
        cnt = {e: 0 for e in COMPUTE}
        dcount = {}
        dk = {"sp": 0, "pool": 0}
        for o in ops:
            if o.dma:
                pool_ = dpools[o.eng]
                o.sem = pool_[dk[o.eng] % len(pool_)]
                dk[o.eng] += 1
                o.pv = dcount.get(id(o.sem), (None, 0))[1]
                o.val = o.pv + 16
                dcount[id(o.sem)] = (o.sem, o.val)
                o.signal = True
            else:
                if o.signal:
                    cnt[o.eng] += 1
                o.sem = sems[o.eng]
                o.val = cnt[o.eng]
        per = {e: [] for e in ALLENG}
        for o in ops:
            per[o.eng].append(o)
        block = stack.enter_context(nc.Block())

        def run(eng_name, engine):
            waited = {}
            for o in per[eng_name]:
                need = {}
                for d in o.deps:
                    p = ops[d]
                    key = id(p.sem)
                    if need.get(key, (None, 0))[1] < p.val:
                        need[key] = (p.sem, p.val)
                if o.dma and o.pv > 0:
                    key = id(o.sem)
                    if need.get(key, (None, 0))[1] < o.pv:
                        need[key] = (o.sem, o.pv)
                for key, (s, v) in need.items():
                    if waited.get(key, 0) < v:
                        engine.wait_ge(s, v)
                        waited[key] = v
                ins = o.fn(engine)
                if o.signal:
                    ins.then_inc(o.sem, 16 if o.dma else 1)
            if eng_name == "sp":
                for key, (sm_, v) in dcount.items():
                    if waited.get(key, 0) < v:
                        engine.wait_ge(sm_, v)

        @block.tensor
        def _(e):
            run("pe", e)

        @block.scalar
        def _(e):
            run("act", e)

        @block.vector
        def _(e):
            run("dve", e)

        @block.gpsimd
        def _(e):
            run("pool", e)

        @block.sync
        def _(e):
            run("sp", e)


class Mem:
    def __init__(self, nc, limit=224 * 1024 - 256):
        self.nc = nc
        self.off = 16 * 1024 + 256
        self.limit = limit
        self.n = 0
        self.work = None
        self.work_base = 0

    def alloc(self, name, shape, dt, at=None):
        esz = 2 if dt == BF16 else 4
        nel = int(np.prod(shape[1:]))
        nbytes = (nel * esz + 63) // 64 * 64
        if at is None:
            at = self.off
            self.off += nbytes
            assert self.off <= self.limit, (name, self.off)
            self.n += 1
            t = self.nc.alloc_sbuf_tensor_at("%s_%d" % (name, self.n), list(shape), dt, offset=at)
            return t, at + nbytes
        if self.work is None:
            self.work_base = self.off
            nwords = (self.limit - self.off) // 4
            self.work = self.nc.alloc_sbuf_tensor_at("work", [128, nwords], F32, offset=self.off)
        assert at + nbytes <= self.limit, (name, at + nbytes)
        w0 = (at - self.work_base) // 4
        v = self.work[:, w0:w0 + (nel * esz + 3) // 4]
        if dt == BF16:
            v = v.bitcast(BF16)[:, 0:nel]
        if len(shape) == 3:
            v = v.rearrange("p (a b) -> p a b", a=shape[1])
        elif len(shape) == 4:
            v = v.rearrange("p (a b c) -> p a b c", a=shape[1], b=shape[2])
        return v, at + nbytes


def build_program(stage=99, dumps=None, SUB=0, MUT=0):
    nc = bass.Bass("TRN2", target_bir_lowering=False)
    dr = lambda name, shape, dt=F32: nc.dram_tensor(name, list(shape), dt, kind="ExternalInput").ap()
    x_d = dr("x", [NB, S_LEN, D])
    cT_d = dr("cT", [128, KC, NB])
    consts_d = dr("consts", [128, 4, 128])
    normw_d = dr("normw", [128, D])
    fnw_d = dr("fnw", [128, D])
    mnw_d = dr("mnw", [128, D])
    anw_d = dr("anw", [128, 128])
    convw_d = dr("convw", [128, 2, KC, 4])
    bif_d = dr("bif", [128, NT, 8])
    lam_d = dr("lam", [128, 4, 64])
    relb_d = dr("relb", [8, 128, 2, 128])
    c31_d = dr("c31", [128, 8])
    bada_d = dr("bada", [128, 3 * D])
    wada_d = dr("w_ada", [D, 3 * D]).rearrange("(k p) j -> p k j", p=128)
    win_d = dr("w_in", [D, D_IN]).rearrange("(k p) j -> p k j", p=128)
    wout_d = dr("w_out", [2 * D, D]).rearrange("(k p) j -> p k j", p=128)
    out_d = nc.dram_tensor("out", [NB, S_LEN, D], F32, kind="ExternalOutput").ap()

    st = ExitStack()
    S = Sched(nc)
    M = Mem(nc)
    A = lambda name, shape, dt=F32: M.alloc(name, shape, dt)[0]

    def dump(name, ap, shape, reads, dt=F32):
        if dumps is None:
            return
        d = nc.dram_tensor("dbg_" + name, list(shape), dt, kind="ExternalOutput").ap()
        dumps[name] = (tuple(shape), dt)
        S.dma(d, ap, reads=reads)

    cst = A("cst", [128, 4, 128])
    identb = A("identb", [128, 128], BF16)
    mhalf = A("mhalf", [128, 16])
    zeros_f = A("zeros_f", [128, 16])
    normw = A("normw", [128, D])
    gt = A("gt", [128, D])
    sht = A("sht", [128, D])
    gatet = A("gatet", [128, D])
    fnw = A("fnw", [128, D])
    mnw = A("mnw", [128, D])
    anw = A("anw", [128, 128])
    convw = A("convw", [128, 2, KC, 4])
    bif = A("bif", [128, NT, 8])
    lamt = A("lamt", [128, 4, 64])
    lams = A("lams", [128, 16])
    c31 = A("c31", [128, 8])
    cT = A("cT", [128, KC, NB])
    cth = A("cth", [128, KC, NB])
    sc2 = A("sc2", [128, KC, NB])
    scB = A("scB", [128, KC, 128])
    wada = A("wada", [128, KC, 256])
    badap = A("badap", [128, 256])
    wg = A("wg", [128, KC, 8], BF16)
    gpre = A("gpre", [128, NT, 8])
    gef = A("gef", [128, NT, 4])
    glf = A("glf", [128, NT, 4])
    gtmp = A("gtmp", [128, NT, 4])
    g_u = A("g_u", [128, NT, 4])
    g_ebs = A("g_ebs", [128, NT, 4])
    g_ebL = A("g_ebL", [128, NT, 4])
    ssq = A("ssq", [128, NT])
    rstd = A("rstd", [128, NT])
    hT = A("hT", [128, KC, S_LEN], BF16)
    hcatT = A("hcatT", [128, 2 * KC, S_LEN], BF16)
    NSLOT = 8
    slots = [A("slot%d" % i, [128, 2048], BF16) for i in range(NSLOT)]
    lprod = A("lprod", [128, 2, 64])
    base = M.off

    ps = [st.enter_context(nc.psum_tensor("ps%d" % i, [128, 512], F32)) for i in range(8)]
    psb = [p[:].bitcast(BF16) for p in ps]

    ident_f = cst[:, 0, :]
    triu_f = cst[:, 1, :]
    ones_f = cst[:, 2, :]
    maskneg = cst[:, 3, :]

    S.dma(cst[:], consts_d, writes=["cst"])
    S.op("dve", lambda e: e.tensor_copy(out=identb[:], in_=ident_f), ["cst"], ["identb"])
    S.op("pool", lambda e: e.memset(mhalf[:], -0.5), [], ["mhalf"])
    S.op("pool", lambda e: e.memset(zeros_f[:], 0.0), [], ["zeros"])
    for t_sb, t_d, nm in ((normw, normw_d, "normw"), (fnw, fnw_d, "fnw"), (mnw, mnw_d, "mnw"), (anw, anw_d, "anw"),
                          (convw, convw_d, "convw"), (bif, bif_d, "bif"), (lamt, lam_d, "lamt"), (c31, c31_d, "c31"),
                          (cT, cT_d, "cT")):
        S.dma(t_sb[:], t_d, writes=[nm])
    S.dma(wg[:], win_d[:, :, MI:MI + 8], writes=["wg"], q="pool")
    S.op("act", lambda e: e.activation(out=cth[:], in_=cT[:], func=AF.Tanh, scale=0.5), ["cT"], ["cth"])
    S.op("dve", lambda e: e.scalar_tensor_tensor(out=sc2[:], in0=cth[:], scalar=1.0, in1=cT[:], op0=ALU.add, op1=ALU.mult),
         ["cth", "cT"], ["sc2"])
    lam_init = 0.8 - 0.6 * math.exp(-0.3 * 0)
    S.op("dve", lambda e: e.tensor_tensor(out=lprod[:, 0, :], in0=lamt[:, 0, :], in1=lamt[:, 1, :], op=ALU.mult), ["lamt"], ["lprod0"])
    S.op("dve", lambda e: e.tensor_tensor(out=lprod[:, 1, :], in0=lamt[:, 2, :], in1=lamt[:, 3, :], op=ALU.mult), ["lamt"], ["lprod1"])
    S.op("dve", lambda e: e.reduce_sum(out=lams[:, 0:1], in_=lprod[:, 0, :], axis=mybir.AxisListType.X), ["lprod0"], ["lams0"])
    S.op("dve", lambda e: e.reduce_sum(out=lams[:, 1:2], in_=lprod[:, 1, :], axis=mybir.AxisListType.X), ["lprod1"], ["lams1"])
    S.op("act", lambda e: e.activation(out=lams[:, 2:4], in_=lams[:, 0:2], func=AF.Exp), ["lams0", "lams1"], ["lams23"])
    S.op("dve", lambda e: e.tensor_tensor(out=lams[:, 4:5], in0=lams[:, 2:3], in1=lams[:, 3:4], op=ALU.subtract), ["lams23"], ["lams4"])
    S.op("dve", lambda e: e.tensor_scalar(out=lams[:, 5:6], in0=lams[:, 4:5], scalar1=lam_init, scalar2=-1.0, op0=ALU.add, op1=ALU.mult),
         ["lams4"], ["lams5"])
    neg_lam = lams[:, 5:6]
    dump("neglam", lams[:, 0:6], [128, 6], ["lams5"])

    ada_bank = [4]

    def adaln(b, which):
        S.op("dve", lambda e: e.tensor_copy(out=scB[:], in_=sc2[:, :, b:b + 1].to_broadcast([128, KC, 128])), ["sc2"], ["scB"])
        pieces = range(0, 8) if which == 0 else range(8, 12)
        for p in pieces:
            j0 = p * 256
            S.dma(wada[:], wada_d[:, :, j0:j0 + 256], writes=["wada"])
            S.dma(badap[:], bada_d[:, j0:j0 + 256], writes=["badap"])
            bk = ada_bank[0]
            ada_bank[0] = 4 + (bk - 4 + 1) % 2
            for kc in range(KC):
                S.op("pe", lambda e, kc=kc, bk=bk: e.matmul(ps[bk][:, 0:256], lhsT=scB[:, kc, :], rhs=wada[:, kc, :],
                                                           start=(kc == 0), stop=(kc == KC - 1)),
                     ["scB", "wada"], [("ps", bk)])
            if p < 4:
                dst = sht[:, j0:j0 + 256]
                S.op("dve", lambda e, bk=bk, dst=dst: e.scalar_tensor_tensor(out=dst, in0=ps[bk][:, 0:256], scalar=0.5, in1=badap[:],
                                                                              op0=ALU.mult, op1=ALU.add),
                     [("ps", bk), "badap"], [("sht", p)])
            elif p < 8:
                c0 = j0 - D
                dst = gt[:, c0:c0 + 256]
                S.op("dve", lambda e, bk=bk, dst=dst: e.scalar_tensor_tensor(out=dst, in0=ps[bk][:, 0:256], scalar=0.5, in1=badap[:],
                                                                              op0=ALU.mult, op1=ALU.add),
                     [("ps", bk), "badap"], [("gt", p - 4)])
                S.op("dve", lambda e, dst=dst, c0=c0: e.scalar_tensor_tensor(out=dst, in0=dst, scalar=1.0, in1=normw[:, c0:c0 + 256],
                                                                              op0=ALU.add, op1=ALU.mult),
                     [("gt", p - 4), "normw"], [("gt", p - 4)])
            else:
                c0 = j0 - 2 * D
                dst = gatet[:, c0:c0 + 256]
                S.op("dve", lambda e, bk=bk, dst=dst: e.scalar_tensor_tensor(out=dst, in0=ps[bk][:, 0:256], scalar=0.5, in1=badap[:],
                                                                              op0=ALU.mult, op1=ALU.add),
                     [("ps", bk), "badap"], [("gatet", p - 8)])

    pieces = []
    for _b in range(NB):
        for hd in range(4):
            for c0 in (MQ, MK, MV, MO, MZ):
                pieces.append((win_d[:, :, c0 + hd * 256:c0 + hd * 256 + 256], KC, 256))
        for pr in range(4):
            for c0 in (AQ, AK, AV, AZ):
                pieces.append((win_d[:, :, c0 + pr * 256:c0 + pr * 256 + 256], KC, 256))
        for i in range(8):
            pieces.append((wout_d[:, 2 * i:2 * i + 2, :], 2, 1024))
    PIECES_PER_SEQ = 20 + 16 + 8
    pstate = {"issued": 0}

    def issue_piece():
        i = pstate["issued"]
        if i >= len(pieces):
            return
        pstate["issued"] += 1
        src, k, n = pieces[i]
        dst = slots[i % NSLOT][:, 0:k * n].rearrange("p (k n) -> p k n", k=k)
        S.dma(dst, src, writes=[("slot", i % NSLOT)], q="pool", ph=False)

    def release(i):
        while pstate["issued"] <= i + NSLOT and pstate["issued"] < len(pieces):
            issue_piece()

    def pview(i, k, n):
        return slots[i % NSLOT][:, 0:k * n].rearrange("p (k n) -> p k n", k=k)

    def ptok(i):
        return ("slot", i % NSLOT)

    rot_state = {"n": 0}

    def rot():
        r = rot_state["n"] % 4
        rot_state["n"] += 1
        return r

    PH = "PHASE"

    def barrier():
        S.op("pool", lambda e: e.memset(mhalf[:, 8:16], -0.5), [], [PH])

    S.op("dve", lambda e: e.tensor_scalar(out=mnw[:], in0=mnw[:], scalar1=0.25, scalar2=None, op0=ALU.mult), ["mnw"], ["mnw4"])
    S.op("dve", lambda e: e.tensor_scalar(out=anw[:], in0=anw[:], scalar1=(1.0 - lam_init) * 0.5, scalar2=None, op0=ALU.mult), ["anw"], ["anw2"])
    for _ in range(NSLOT):
        issue_piece()

    def seq_body(b):
        if stage < 1:
            return
        adaln(b, 0)
        barrier()
        off = base
        xt = []
        tmpf = []
        hb = []
        for i in range(2):
            t_, off = M.alloc("xt", [128, D], F32, at=off); xt.append(t_)
            t_, off = M.alloc("tmpf", [128, D], F32, at=off); tmpf.append(t_)
            t_, off = M.alloc("hb", [128, D], BF16, at=off); hb.append(t_)
        for tt in range(NT):
            i = tt % 2
            S.dma(xt[i][:], x_d[b, tt * 128:(tt + 1) * 128, :], reads=[PH], writes=[("xt", i)])
            S.op("act", lambda e, i=i, tt=tt: e.activation(out=tmpf[i][:], in_=xt[i][:], func=AF.Square, scale=1.0 / 32.0,
                                                          accum_out=ssq[:, tt:tt + 1]),
                 [("xt", i), PH], [("tmpf", i), ("ssq", tt)])
            S.op("pool", lambda e, tt=tt: e.tensor_scalar(out=rstd[:, tt:tt + 1], in0=ssq[:, tt:tt + 1], scalar1=EPS, scalar2=1.0,
                                                          op0=ALU.add, op1=ALU.mult), [("ssq", tt)], [("rstd", tt)])
            S.op("pool", lambda e, tt=tt: e.tensor_tensor(out=rstd[:, tt:tt + 1], in0=rstd[:, tt:tt + 1], in1=mhalf[:, 0:1], op=ALU.pow),
                 [("rstd", tt), "mhalf"], [("rstd", tt)])
            S.op("dve", lambda e, i=i, tt=tt: e.scalar_tensor_tensor(out=tmpf[i][:], in0=xt[i][:], scalar=rstd[:, tt:tt + 1], in1=gt[:],
                                                                      op0=ALU.mult, op1=ALU.mult),
                 [("xt", i), ("rstd", tt), PH] + [("gt", q) for q in range(4)], [("tmpf", i)])
            S.op("pool", lambda e, i=i: e.tensor_tensor(out=hb[i][:], in0=tmpf[i][:], in1=sht[:], op=ALU.add),
                 [("tmpf", i), PH] + [("sht", q) for q in range(4)], [("hb", i)])
            bk = 6 + tt % 2
            for kc in range(KC):
                S.op("pe", lambda e, i=i, kc=kc, bk=bk: e.transpose(psb[bk][:, kc * 128:(kc + 1) * 128], hb[i][:, kc * 128:(kc + 1) * 128], identb[:]),
                     [("hb", i), "identb", PH], [("ps", bk)])
            S.op("act", lambda e, tt=tt, bk=bk: e.activation(out=hT[:, :, tt * 128:(tt + 1) * 128],
                                                            in_=psb[bk][:].rearrange("p (k t) -> p k t", k=KC), func=AF.Copy),
                 [("ps", bk)], [("hT", tt)])
        if b == 0:
            dump("gt", gt[:], [128, D], [("gt", q) for q in range(4)])
            dump("sht", sht[:], [128, D], [("sht", q) for q in range(4)])
            dump("hT", hT[:], [128, KC, S_LEN], [("hT", q) for q in range(NT)], BF16)
        if stage < 2:
            return
        hT_all = [("hT", q) for q in range(NT)]
        for tt in range(NT):
            for kc in range(KC):
                S.op("pe", lambda e, tt=tt, kc=kc: e.matmul(ps[2][:, tt * 8:(tt + 1) * 8], lhsT=hT[:, kc, tt * 128:(tt + 1) * 128],
                                                           rhs=wg[:, kc, :], start=(kc == 0), stop=(kc == KC - 1)),
                     [("hT", tt), "wg"], [("ps", 2)])
        S.op("dve", lambda e: e.tensor_tensor(out=gpre[:], in0=ps[2][:, 0:128].rearrange("p (t g) -> p t g", g=8), in1=bif[:], op=ALU.add),
             [("ps", 2), "bif"], ["gpre"])
        S.op("act", lambda e: e.activation(out=gef[:], in_=gpre[:, :, 4:8], func=AF.Exp, scale=-1.0), ["gpre"], ["gef"])
        S.op("act", lambda e: e.activation(out=glf[:], in_=gef[:], func=AF.Ln, bias=1.0), ["gef"], ["glf"])
        glf2 = glf[:].rearrange("p t g -> p (t g)")
        S.op("pe", lambda e: e.matmul(ps[3][:, 0:64], lhsT=triu_f, rhs=glf2, start=True, stop=True), ["glf", "cst"], [("ps", 3)])
        S.op("pe", lambda e: e.matmul(ps[3][:, 64:128], lhsT=ones_f, rhs=glf2, start=True, stop=True), ["glf", "cst"], [("ps", 3)])
        cum = ps[3][:, 0:64].rearrange("p (t g) -> p t g", g=4)
        tot = ps[3][:, 64:128].rearrange("p (t g) -> p t g", g=4)
        S.op("dve", lambda e: e.tensor_tensor(out=gtmp[:], in0=gpre[:, :, 0:4], in1=cum, op=ALU.add), ["gpre", ("ps", 3)], ["gtmp"])
        S.op("act", lambda e: e.activation(out=g_u[:], in_=gtmp[:], func=AF.Exp), ["gtmp"], ["g_u"])
        S.op("act", lambda e: e.activation(out=g_ebs[:], in_=cum, func=AF.Exp, scale=-1.0, bias=math.log(1.0 / 64.0)),
             [("ps", 3)], ["g_ebs"])
        S.op("act", lambda e: e.activation(out=g_ebL[:], in_=tot, func=AF.Exp, scale=-1.0), [("ps", 3)], ["g_ebL"])
        if b == 0:
            dump("g_u", g_u[:], [128, NT, 4], ["g_u"])
            dump("g_ebs", g_ebs[:], [128, NT, 4], ["g_ebs"])
            dump("g_ebL", g_ebL[:], [128, NT, 4], ["g_ebL"])
        if stage < 3:
            return
        pbase = b * PIECES_PER_SEQ
        def mhead(hd):
            barrier()
            off = base
            qT, off = M.alloc("qT", [128, 2, S_LEN], BF16, at=off)
            kT, off = M.alloc("kT", [128, 2, S_LEN], BF16, at=off)
            base2 = off
            qkT = (qT, kT)
            pre = []; acc = []; thb = []
            for i in range(2):
                t_, off = M.alloc("pre", [128, 515], F32, at=off); pre.append(t_)
                t_, off = M.alloc("acc", [128, 512], F32, at=off); acc.append(t_)
                t_, off = M.alloc("thb", [128, 512], F32, at=off); thb.append(t_)
            halo, off = M.alloc("halo", [128, 4, 4], F32, at=off)
            p_q = pbase + hd * 5
            n = 0
            for qk in range(2):
                pi_ = p_q + qk
                wv = pview(pi_, KC, 256)
                for dc in range(2):
                    dcg = hd * 2 + dc
                    for tg in range(4):
                        bk = rot()
                        pi = n % 2
                        n += 1
                        for kc in range(KC):
                            S.op("pe", lambda e, bk=bk, wv=wv, kc=kc, dc=dc, tg=tg: e.matmul(
                                ps[bk][:, 0:512], lhsT=wv[:, kc, dc * 128:(dc + 1) * 128], rhs=hT[:, kc, tg * 512:(tg + 1) * 512],
                                start=(kc == 0), stop=(kc == KC - 1)),
                                [ptok(pi_)] + hT_all[4 * tg:4 * tg + 4], [("ps", bk)])
                        S.op("act", lambda e, bk=bk, pi=pi: e.activation(out=pre[pi][:, 3:515], in_=ps[bk][:, 0:512], func=AF.Copy),
                             [("ps", bk), PH], [("pre", pi)])
                        hl = halo[:, qk * 2 + dc, 0:3]
                        if tg == 0:
                            S.op("pool", lambda e, pi=pi: e.memset(pre[pi][:, 0:3], 0.0), [PH], [("preh", pi)])
                        else:
                            S.op("pool", lambda e, pi=pi, hl=hl: e.tensor_copy(out=pre[pi][:, 0:3], in_=hl), [("halo", qk, dc), PH], [("preh", pi)])
                        S.op("pool", lambda e, pi=pi, hl=hl: e.tensor_copy(out=hl, in_=pre[pi][:, 512:515]), [("pre", pi), PH], [("halo", qk, dc)])
                        cw = lambda j, qk=qk, dcg=dcg: convw[:, qk, dcg, j:j + 1]
                        S.op("dve", lambda e, pi=pi, cw=cw: e.tensor_scalar(out=acc[pi][:], in0=pre[pi][:, 3:515], scalar1=cw(3), scalar2=None, op0=ALU.mult),
                             [("pre", pi), "convw", PH], [("acc", pi)])
                        for j in (2, 1, 0):
                            S.op("dve", lambda e, pi=pi, cw=cw, j=j: e.scalar_tensor_tensor(out=acc[pi][:], in0=pre[pi][:, j:j + 512], scalar=cw(j),
                                                                                         in1=acc[pi][:], op0=ALU.mult, op1=ALU.add),
                                 [("pre", pi), ("preh", pi), "convw", ("acc", pi), PH], [("acc", pi)])
                        S.op("act", lambda e, pi=pi: e.activation(out=thb[pi][:], in_=acc[pi][:], func=AF.Tanh, scale=0.5),
                             [("acc", pi), PH], [("thb", pi)])
                        S.op("pool", lambda e, pi=pi: e.tensor_scalar(out=thb[pi][:], in0=thb[pi][:], scalar1=1.0, scalar2=1.0, op0=ALU.add, op1=ALU.mult),
                             [("thb", pi), PH], [("thb", pi)])
                        dst = qkT[qk][:, dc, tg * 512:(tg + 1) * 512]
                        S.op("pool", lambda e, pi=pi, dst=dst: e.tensor_tensor(out=dst, in0=thb[pi][:], in1=acc[pi][:], op=ALU.mult),
                             [("thb", pi), ("acc", pi), PH], [("qkT", qk, dc, tg)])
                release(pi_)
            if b == 0 and hd == 0:
                dump("m_qT", qT[:], [128, 2, S_LEN], [("qkT", 0, dc, tg) for dc in range(2) for tg in range(4)], BF16)
                dump("m_kT", kT[:], [128, 2, S_LEN], [("qkT", 1, dc, tg) for dc in range(2) for tg in range(4)], BF16)
            if SUB == 1:
                return
            barrier()
            off = base2
            uv = []; Gp = []; kTok = []; maskS = []; Hn = []; hmb = []; sm = []; stats = []; mv = []
            for i in range(2):
                t_, off = M.alloc("uv", [128, WA], BF16, at=off); uv.append(t_)
                t_, off = M.alloc("Gp", [128, 256], F32, at=off); Gp.append(t_)
                t_, off = M.alloc("kTok", [128, 256], BF16, at=off); kTok.append(t_)
                t_, off = M.alloc("maskS", [128, 128], BF16, at=off); maskS.append(t_)
                t_, off = M.alloc("Hn", [128, 256], F32, at=off); Hn.append(t_)
                t_, off = M.alloc("hmb", [128, 256], BF16, at=off); hmb.append(t_)
                t_, off = M.alloc("sm", [128, 16], F32, at=off); sm.append(t_)
                t_, off = M.alloc("stats", [128, 6], F32, at=off); stats.append(t_)
                t_, off = M.alloc("mv", [128, 2], F32, at=off); mv.append(t_)
            to_, off = M.alloc("to", [128, 256], F32, at=off)
            tz_, off = M.alloc("tz", [128, 256], F32, at=off)
            zs_, off = M.alloc("zs", [128, 256], F32, at=off)
            Sf, off = M.alloc("Sf", [128, 128], F32, at=off)
            dCs, off = M.alloc("dCs", [128, 2, WA], F32, at=off)
            Mst, off = M.alloc("Mst", [128, 2, WA], F32, at=off)
            Cb, off = M.alloc("Cb", [128, 2, WA], BF16, at=off)
            assert off <= M.limit, off
            p_v, p_o, p_z = p_q + 2, p_q + 3, p_q + 4
            wvv, wvo, wvz = pview(p_v, KC, 256), pview(p_o, KC, 256), pview(p_z, KC, 256)
            mn4 = mnw[:, hd * 256:(hd + 1) * 256]
            for i in range(2):
                S.op("dve", lambda e, i=i: e.tensor_copy(out=uv[i][:, 256:WA], in_=zeros_f[:, 0:WA - 256]), ["zeros"], [("uv1", i)])
            for c in range(NT):
                ci = c % 2
                tsl = slice(c * 128, (c + 1) * 128)
                bA, bB = rot(), rot()
                for (wv_, pt_, bk, c0) in ((wvv, p_v, bA, 0), (wvo, p_o, bA, 256), (wvz, p_z, bB, 0)):
                    for kc in range(KC):
                        S.op("pe", lambda e, wv_=wv_, bk=bk, c0=c0, kc=kc, tsl=tsl: e.matmul(
                            ps[bk][:, c0:c0 + 256], lhsT=hT[:, kc, tsl], rhs=wv_[:, kc, :], start=(kc == 0), stop=(kc == KC - 1)),
                            [ptok(pt_), ("hT", c)], [("ps", bk)])
                u_c = g_u[:, c, hd:hd + 1]
                S.op("act", lambda e, ci=ci, bA=bA, u_c=u_c: e.activation(out=uv[ci][:, 0:256], in_=ps[bA][:, 0:256], func=AF.Copy, scale=u_c),
                     [("ps", bA), "g_u", PH], [("uv", ci)])
                S.op("act", lambda e, ci=ci, u_c=u_c: e.activation(out=uv[ci][:, 256:257], in_=u_c, func=AF.Copy), ["g_u", PH], [("uv1", ci)])
                S.op("act", lambda e, bA=bA: e.activation(out=to_[:], in_=ps[bA][:, 256:512], func=AF.Tanh, scale=0.5), [("ps", bA), PH], ["to"])
                S.op("act", lambda e, bB=bB: e.activation(out=tz_[:], in_=ps[bB][:, 0:256], func=AF.Tanh, scale=0.5), [("ps", bB), PH], ["tz"])
                S.op("act", lambda e, bB=bB: e.activation(out=zs_[:], in_=ps[bB][:, 0:256], func=AF.Copy), [("ps", bB), PH], ["zs"])
                S.op("dve", lambda e, ci=ci: e.scalar_tensor_tensor(out=Gp[ci][:], in0=to_[:], scalar=1.0, in1=zs_[:], op0=ALU.add, op1=ALU.mult),
                     ["to", "zs", PH], [("Gp", ci)])
                S.op("dve", lambda e, ci=ci: e.scalar_tensor_tensor(out=Gp[ci][:], in0=tz_[:], scalar=1.0, in1=Gp[ci][:], op0=ALU.add, op1=ALU.mult),
                     ["tz", ("Gp", ci), PH], [("Gp", ci)])
                S.op("pool", lambda e, ci=ci: e.tensor_tensor(out=Gp[ci][:], in0=Gp[ci][:], in1=mn4, op=ALU.mult), [("Gp", ci), "mnw4", PH], [("Gp", ci)])
                S.mark("A%d" % c)
                for dc in range(2):
                    S.op("pe", lambda e, dc=dc, tsl=tsl: e.transpose(psb[4][:, dc * 128:(dc + 1) * 128], kT[:, dc, tsl], identb[:]),
                         [("qkT", 1, dc, c // 4), "identb"], [("ps", 4)])
                S.op("act", lambda e, ci=ci: e.activation(out=kTok[ci][:], in_=psb[4][:, 0:256], func=AF.Copy), [("ps", 4), PH], [("kTok", ci)])
                for dc in range(2):
                    S.op("pe", lambda e, dc=dc, tsl=tsl: e.matmul(ps[5][:, 0:128], lhsT=kT[:, dc, tsl], rhs=qT[:, dc, tsl], start=(dc == 0), stop=(dc == 1)),
                         [("qkT", 1, dc, c // 4), ("qkT", 0, dc, c // 4)], [("ps", 5)])
                S.op("act", lambda e: e.activation(out=Sf[:], in_=ps[5][:, 0:128], func=AF.Copy), [("ps", 5), PH], ["Sf"])
                S.op("pool", lambda e, ci=ci: e.tensor_tensor(out=maskS[ci][:], in0=Sf[:], in1=triu_f, op=ALU.mult),
                     ["Sf", "cst", PH], [("maskS", ci)])
                S.mark("B%d" % c)
                if c > 0:
                    for dc in range(2):
                        S.op("pe", lambda e, dc=dc, tsl=tsl: e.matmul(ps[6][:, 0:WA], lhsT=qT[:, dc, tsl], rhs=Cb[:, dc, :], start=(dc == 0), stop=False),
                             [("qkT", 0, dc, c // 4), "Cb"], [("ps", 6)])
                S.op("pe", lambda e, ci=ci, c=c: e.matmul(ps[6][:, 0:WA], lhsT=maskS[ci][:], rhs=uv[ci][:], start=(c == 0), stop=True),
                     [("maskS", ci), ("uv", ci), ("uv1", ci)], [("ps", 6)])
                ebs_c = g_ebs[:, c, hd:hd + 1]
                smc = sm[ci]
                S.op("act", lambda e, smc=smc: e.activation(out=smc[:, 7:8], in_=ps[6][:, 256:257], func=AF.Copy), [("ps", 6), PH], [("sm", ci, 7)])
                S.op("dve", lambda e, smc=smc, ebs_c=ebs_c: e.tensor_tensor(out=smc[:, 0:1], in0=smc[:, 7:8], in1=ebs_c, op=ALU.mult),
                     [("sm", ci, 7), "g_ebs", PH], [("sm", ci, 0)])
                S.op("dve", lambda e, smc=smc: e.tensor_tensor(out=smc[:, 1:2], in0=smc[:, 0:1], in1=smc[:, 0:1], op=ALU.mult), [("sm", ci, 0), PH], [("sm", ci, 1)])
                S.op("dve", lambda e, smc=smc: e.tensor_scalar(out=smc[:, 2:3], in0=smc[:, 1:2], scalar1=1.0, scalar2=None, op0=ALU.max), [("sm", ci, 1), PH], [("sm", ci, 2)])
                S.op("pool", lambda e, smc=smc: e.tensor_tensor(out=smc[:, 3:4], in0=smc[:, 2:3], in1=mhalf[:, 0:1], op=ALU.pow), [("sm", ci, 2), "mhalf", PH], [("sm", ci, 3)])
                S.op("dve", lambda e, smc=smc, ebs_c=ebs_c: e.tensor_tensor(out=smc[:, 4:5], in0=smc[:, 3:4], in1=ebs_c, op=ALU.mult), [("sm", ci, 3), "g_ebs", PH], [("sm", ci, 4)])
                S.op("act", lambda e, ci=ci, smc=smc: e.activation(out=Hn[ci][:], in_=ps[6][:, 0:256], func=AF.Copy, scale=smc[:, 4:5]),
                     [("ps", 6), ("sm", ci, 4), PH], [("Hn", ci)])
                S.mark("C%d" % c)
                S.op("dve", lambda e, ci=ci: e.bn_stats(out=stats[ci][:], in_=Hn[ci][:]), [("Hn", ci), PH], [("stats", ci)])
                S.op("dve", lambda e, ci=ci: e.bn_aggr(out=mv[ci][:], in_=stats[ci][:]), [("stats", ci), PH], [("mv", ci)])
                S.op("pool", lambda e, ci=ci, smc=smc: e.tensor_scalar(out=smc[:, 5:6], in0=mv[ci][:, 1:2], scalar1=EPS, scalar2=1.0, op0=ALU.add, op1=ALU.mult),
                     [("mv", ci), PH], [("sm", ci, 5)])
                S.op("pool", lambda e, smc=smc: e.tensor_tensor(out=smc[:, 6:7], in0=smc[:, 5:6], in1=mhalf[:, 0:1], op=ALU.pow), [("sm", ci, 5), "mhalf", PH], [("sm", ci, 6)])
                S.op("dve", lambda e, ci=ci, smc=smc: e.tensor_scalar(out=Hn[ci][:], in0=Hn[ci][:], scalar1=mv[ci][:, 0:1], scalar2=smc[:, 6:7],
                                                                      op0=ALU.subtract, op1=ALU.mult),
                     [("Hn", ci), ("mv", ci), ("sm", ci, 6), PH], [("Hn", ci)])
                S.op("pool", lambda e, ci=ci: e.tensor_tensor(out=hmb[ci][:], in0=Hn[ci][:], in1=Gp[ci][:], op=ALU.mult), [("Hn", ci), ("Gp", ci), PH], [("hmb", ci)])
                bT = rot()
                for dc in range(2):
                    S.op("pe", lambda e, ci=ci, dc=dc, bT=bT: e.transpose(psb[bT][:, dc * 128:(dc + 1) * 128], hmb[ci][:, dc * 128:(dc + 1) * 128], identb[:]),
                         [("hmb", ci), "identb"], [("ps", bT)])
                S.op("act", lambda e, tsl=tsl, bT=bT: e.activation(out=hcatT[:, hd * 2:hd * 2 + 2, tsl], in_=psb[bT][:, 0:256].rearrange("p (k t) -> p k t", k=2), func=AF.Copy),
                     [("ps", bT)], [("hcatT", hd * 2, c), ("hcatT", hd * 2 + 1, c)])
                S.mark("D%d" % c)
                if c < NT - 1:
                    S.op("pe", lambda e, ci=ci: e.matmul(ps[7][:, 0:WA], lhsT=kTok[ci][:, 0:128], rhs=uv[ci][:], start=True, stop=True),
                         [("kTok", ci), ("uv", ci), ("uv1", ci)], [("ps", 7)])
                    bD = rot()
                    S.op("pe", lambda e, ci=ci, bD=bD: e.matmul(ps[bD][:, 0:WA], lhsT=kTok[ci][:, 128:256], rhs=uv[ci][:], start=True, stop=True),
                         [("kTok", ci), ("uv", ci), ("uv1", ci)], [("ps", bD)])
                    for dc, (src, tk) in enumerate(((ps[7][:, 0:WA], ("ps", 7)), (ps[bD][:, 0:WA], ("ps", bD)))):
                        S.op("act", lambda e, dc=dc, src=src: e.activation(out=dCs[:, dc, :], in_=src, func=AF.Copy), [tk, PH], [("dCs", dc)])
                        if c == 0:
                            S.op("dve", lambda e, dc=dc: e.tensor_copy(out=Mst[:, dc, :], in_=dCs[:, dc, :]), [("dCs", dc), PH, "Cb"], [("Mst", dc)])
                        else:
                            ebp = g_ebL[:, c - 1, hd:hd + 1]
                            S.op("dve", lambda e, dc=dc, ebp=ebp: e.scalar_tensor_tensor(out=Mst[:, dc, :], in0=Mst[:, dc, :], scalar=ebp, in1=dCs[:, dc, :],
                                                                                         op0=ALU.mult, op1=ALU.add),
                                 [("dCs", dc), ("Mst", dc), "g_ebL", PH], [("Mst", dc)])
                    ebc = g_ebL[:, c, hd:hd + 1]
                    S.op("act", lambda e, ebc=ebc: e.activation(out=Cb[:], in_=Mst[:], func=AF.Copy, scale=ebc), [("Mst", 0), ("Mst", 1), "g_ebL", PH], ["Cb"])
                S.mark("E%d" % c)
            release(p_v); release(p_o); release(p_z)

        for hd in range(4 if stage >= 4 else 1):
            mhead(hd)
        if b == 0 and SUB != 1:
            nhd = 8 if stage >= 4 else 2
            dump("hcat_m", hcatT[:, 0:nhd, :], [128, nhd, S_LEN], [("hcatT", k, c) for k in range(nhd) for c in range(NT)], BF16)
        if stage < 5:
            return
        def apair(pr):
            p_aq = pbase + 20 + pr * 4
            p_ak, p_av, p_az = p_aq + 1, p_aq + 2, p_aq + 3
            def ahead(hh):
                ah = pr * 2 + hh
                barrier()
                off = base
                qTa, off = M.alloc("qTa", [128, S_LEN], BF16, at=off)
                kTa, off = M.alloc("kTa", [128, S_LEN], BF16, at=off)
                Vaug, off = M.alloc("Vaug", [128, NT, WV], BF16, at=off)
                Gz = []; Pb = []; tmpS = []; tza = []; t1 = []; t0 = []; zsa = []; ha = []; hab = []; sma = []
                for i in range(2):
                    t_, off = M.alloc("Gz", [128, 4, 128], F32, at=off); Gz.append(t_)
                    t_, off = M.alloc("tmpS", [128, 256], F32, at=off); tmpS.append(t_)
                    t_, off = M.alloc("tza", [128, 128], F32, at=off); tza.append(t_)
                    t_, off = M.alloc("t1", [128, 128], F32, at=off); t1.append(t_)
                    t_, off = M.alloc("t0", [128, 128], F32, at=off); t0.append(t_)
                    t_, off = M.alloc("zsa", [128, 128], F32, at=off); zsa.append(t_)
                    t_, off = M.alloc("ha", [128, 128], F32, at=off); ha.append(t_)
                    t_, off = M.alloc("hab", [128, 128], BF16, at=off); hab.append(t_)
                    t_, off = M.alloc("sma", [128, 16], F32, at=off); sma.append(t_)
                for i in range(3):
                    t_, off = M.alloc("Pb", [128, 512], BF16, at=off); Pb.append(t_)
                relb, off = M.alloc("relb", [128, 2, 128], F32, at=off)
                junk, off = M.alloc("junk", [128, 128], F32, at=off)
                assert off <= M.limit, off
                csl = slice(hh * 128, (hh + 1) * 128)
                wq, wk, wv_, wz = pview(p_aq, KC, 256), pview(p_ak, KC, 256), pview(p_av, KC, 256), pview(p_az, KC, 256)
                S.dma(relb[:], relb_d[ah], writes=["relb"])
                S.op("dve", lambda e: e.tensor_tensor(out=relb[:, 0, :], in0=relb[:, 0, :], in1=maskneg, op=ALU.add), ["relb", "cst"], ["relb"])
                S.op("act", lambda e: e.activation(out=relb[:], in_=relb[:], func=AF.Exp), ["relb"], ["relb"])
                S.op("dve", lambda e: e.tensor_copy(out=Vaug[:, :, 128:WV], in_=ones_f[:, 0:NT * (WV - 128)].rearrange("p (a b) -> p a b", a=NT)), ["cst"], ["Vones"])
                for isk, (w_, pt_, dstT) in enumerate(((wq, p_aq, qTa), (wk, p_ak, kTa))):
                    for tg in range(4):
                        bk = rot()
                        for kc in range(KC):
                            S.op("pe", lambda e, bk=bk, w_=w_, kc=kc, tg=tg: e.matmul(
                                ps[bk][:, 0:512], lhsT=w_[:, kc, csl], rhs=hT[:, kc, tg * 512:(tg + 1) * 512], start=(kc == 0), stop=(kc == KC - 1)),
                                [ptok(pt_)] + hT_all[4 * tg:4 * tg + 4], [("ps", bk)])
                        dst = dstT[:, tg * 512:(tg + 1) * 512]
                        if isk == 0:
                            S.op("act", lambda e, bk=bk, dst=dst: e.activation(out=dst, in_=ps[bk][:, 0:512], func=AF.Copy, scale=0.125),
                                 [("ps", bk)], [("qTa", tg)])
                        else:
                            S.op("act", lambda e, bk=bk, dst=dst: e.activation(out=dst, in_=ps[bk][:, 0:512], func=AF.Copy), [("ps", bk)], [("kTa", tg)])
                c31h = c31[:, ah:ah + 1]
                np_ = 0
                S.mark("AA")
                for G in range(4):
                    gi = G % 2
                    for tt in range(4 * G, 4 * G + 4):
                        bk = rot()
                        tsl = slice(tt * 128, (tt + 1) * 128)
                        for (w_, pt_, c0) in ((wv_, p_av, 0), (wz, p_az, 128)):
                            for kc in range(KC):
                                S.op("pe", lambda e, bk=bk, w_=w_, kc=kc, c0=c0, tsl=tsl: e.matmul(
                                    ps[bk][:, c0:c0 + 128], lhsT=hT[:, kc, tsl], rhs=w_[:, kc, csl], start=(kc == 0), stop=(kc == KC - 1)),
                                    [ptok(pt_), ("hT", tt)], [("ps", bk)])
                        S.op("act", lambda e, bk=bk, tt=tt: e.activation(out=Vaug[:, tt, 0:128], in_=ps[bk][:, 0:128], func=AF.Copy), [("ps", bk)], [("Vaug", tt)])
                        ti = tt % 2
                        S.op("act", lambda e, bk=bk, ti=ti: e.activation(out=tza[ti][:], in_=ps[bk][:, 128:256], func=AF.Tanh, scale=0.5), [("ps", bk)], [("tza", ti)])
                        gdst = Gz[gi][:, tt % 4, :]
                        S.op("act", lambda e, bk=bk, ti=ti: e.activation(out=zsa[ti][:], in_=ps[bk][:, 128:256], func=AF.Copy), [("ps", bk)], [("zsa", ti)])
                        S.op("dve", lambda e, ti=ti, gdst=gdst: e.scalar_tensor_tensor(out=gdst, in0=tza[ti][:], scalar=1.0, in1=zsa[ti][:], op0=ALU.add, op1=ALU.mult),
                             [("tza", ti), ("zsa", ti)], [("Gz", gi, tt % 4)])
                        S.op("pool", lambda e, gdst=gdst: e.tensor_tensor(out=gdst, in0=gdst, in1=anw[:], op=ALU.mult), [("Gz", gi, tt % 4), "anw2"], [("Gz", gi, tt % 4)])
                    S.mark("AB")
                    for m in range(2):
                        msl = slice(m * 64, (m + 1) * 64)
                        for j in range(4 * G + 4):
                            qlo = max(j, 4 * G)
                            nq = 4 * G + 4 - qlo
                            N = nq * 128
                            bk = rot()
                            S.op("pe", lambda e, bk=bk, msl=msl, j=j, qlo=qlo, N=N: e.matmul(
                                ps[bk][:, 0:N], lhsT=kTa[msl, j * 128:(j + 1) * 128], rhs=qTa[msl, qlo * 128:qlo * 128 + N], start=True, stop=True),
                                [("kTa", j // 4)] + [("qTa", G)], [("ps", bk)])
                            pi = np_ % 3
                            np_ += 1
                            nnear = max(0, min(nq, j + 2 - qlo))
                            if nnear > 0:
                                ty0 = qlo - j
                                tsv = tmpS[pi % 2]
                                S.op("act", lambda e, bk=bk, tsv=tsv, nnear=nnear: e.activation(out=tsv[:, 0:nnear * 128], in_=ps[bk][:, 0:nnear * 128], func=AF.Exp),
                                     [("ps", bk)], [("tmpS", pi % 2)])
                                S.op("dve", lambda e, pi=pi, tsv=tsv, nnear=nnear, ty0=ty0: e.tensor_tensor(
                                    out=Pb[pi][:, 0:nnear * 128], in0=tsv[:, 0:nnear * 128],
                                    in1=relb[:, ty0:ty0 + nnear, :].rearrange("p a q -> p (a q)"), op=ALU.mult),
                                    [("tmpS", pi % 2), "relb"], [("P", pi, "n")])
                            if nq > nnear:
                                S.op("act", lambda e, pi=pi, bk=bk, nnear=nnear, N=N: e.activation(out=Pb[pi][:, nnear * 128:N], in_=ps[bk][:, nnear * 128:N],
                                                                                                func=AF.Exp, bias=c31h),
                                     [("ps", bk), "c31"], [("P", pi, "f")])
                            for qi in range(nq):
                                qb = qlo + qi
                                ql = qb - 4 * G
                                abk = 4 + m * 2 + ql // 2
                                col0 = (ql % 2) * 256
                                S.op("pe", lambda e, pi=pi, qi=qi, abk=abk, col0=col0, j=j, qb=qb, ql=ql: e.matmul(
                                    ps[abk][:, col0:col0 + WV], lhsT=Pb[pi][:, qi * 128:(qi + 1) * 128], rhs=Vaug[:, j, :],
                                    start=(j == 0 and ql % 2 == 0), stop=(j == qb and ql % 2 == 1)),
                                    [("P", pi, "n"), ("P", pi, "f"), ("Vaug", j), "Vones"], [("ps", abk)])
                    S.mark("AC")
                    for ql in range(4):
                        qb = 4 * G + ql
                        fi = ql % 2
                        a0 = ps[4 + ql // 2][:, (ql % 2) * 256:(ql % 2) * 256 + 129]
                        a1 = ps[6 + ql // 2][:, (ql % 2) * 256:(ql % 2) * 256 + 129]
                        tk0 = ("ps", 4 + ql // 2)
                        tk1 = ("ps", 6 + ql // 2)
                        sm_ = sma[fi]
                        S.op("act", lambda e, sm_=sm_, a0=a0: e.activation(out=sm_[:, 6:7], in_=a0[:, 128:129], func=AF.Copy), [tk0], [("sma", fi, 6)])
                        S.op("act", lambda e, sm_=sm_, a1=a1: e.activation(out=sm_[:, 7:8], in_=a1[:, 128:129], func=AF.Copy), [tk1], [("sma", fi, 7)])
                        S.op("dve", lambda e, sm_=sm_: e.reciprocal(out=sm_[:, 0:1], in_=sm_[:, 6:7]), [("sma", fi, 6)], [("sma", fi, 0)])
                        S.op("dve", lambda e, sm_=sm_: e.reciprocal(out=sm_[:, 1:2], in_=sm_[:, 7:8]), [("sma", fi, 7)], [("sma", fi, 1)])
                        S.op("dve", lambda e, sm_=sm_: e.tensor_tensor(out=sm_[:, 2:3], in0=sm_[:, 1:2], in1=neg_lam, op=ALU.mult), [("sma", fi, 1), "lams5"], [("sma", fi, 2)])
                        S.op("act", lambda e, fi=fi, sm_=sm_, a1=a1: e.activation(out=t1[fi][:], in_=a1[:, 0:128], func=AF.Copy, scale=sm_[:, 2:3]),
                             [tk1, ("sma", fi, 2)], [("t1", fi)])
                        S.op("act", lambda e, fi=fi, sm_=sm_, a0=a0: e.activation(out=t0[fi][:], in_=a0[:, 0:128], func=AF.Copy, scale=sm_[:, 0:1]),
                             [tk0, ("sma", fi, 0)], [("t0", fi)])
                        S.op("dve", lambda e, fi=fi: e.tensor_tensor(out=ha[fi][:], in0=t0[fi][:], in1=t1[fi][:], op=ALU.add),
                             [("t0", fi), ("t1", fi)], [("ha", fi)])
                        S.op("act", lambda e, fi=fi, sm_=sm_: e.activation(out=junk[:], in_=ha[fi][:], func=AF.Square, scale=128.0 ** -0.5, accum_out=sm_[:, 3:4]),
                             [("ha", fi)], ["junk", ("sma", fi, 3)])
                        S.op("pool", lambda e, sm_=sm_: e.tensor_scalar(out=sm_[:, 4:5], in0=sm_[:, 3:4], scalar1=EPS, scalar2=1.0, op0=ALU.add, op1=ALU.mult),
                             [("sma", fi, 3)], [("sma", fi, 4)])
                        S.op("pool", lambda e, sm_=sm_: e.tensor_tensor(out=sm_[:, 5:6], in0=sm_[:, 4:5], in1=mhalf[:, 0:1], op=ALU.pow), [("sma", fi, 4), "mhalf"], [("sma", fi, 5)])
                        S.op("dve", lambda e, fi=fi, sm_=sm_, ql=ql, gi=gi: e.scalar_tensor_tensor(out=hab[fi][:], in0=ha[fi][:], scalar=sm_[:, 5:6], in1=Gz[gi][:, ql, :],
                                                                                             op0=ALU.mult, op1=ALU.mult),
                             [("ha", fi), ("sma", fi, 5), ("Gz", gi, ql)], [("hab", fi)])
                        bk = rot()
                        S.op("pe", lambda e, fi=fi, bk=bk: e.transpose(psb[bk][:, 0:128], hab[fi][:], identb[:]), [("hab", fi), "identb"], [("ps", bk)])
                        S.op("act", lambda e, bk=bk, qb=qb: e.activation(out=hcatT[:, 8 + ah, qb * 128:(qb + 1) * 128], in_=psb[bk][:, 0:128], func=AF.Copy),
                             [("ps", bk)], [("hcatT", 8 + ah, qb)])
            for hh in range(2):
                ahead(hh)
                S.mark("AD")
            release(p_aq); release(p_ak); release(p_av); release(p_az)

        for pr in range(4 if stage >= 6 else 1):
            apair(pr)
        if b == 0:
            nha = 8 if stage >= 6 else 2
            dump("hcat_a", hcatT[:, 8:8 + nha, :], [128, nha, S_LEN], [("hcatT", k, c) for k in range(8, 8 + nha) for c in range(NT)], BF16)
        if stage < 7:
            return
        adaln(b, 1)
        barrier()
        off = base
        xt5 = []; tmpf5 = []; ot5 = []
        for i in range(2):
            t_, off = M.alloc("xt5", [128, D], F32, at=off); xt5.append(t_)
            t_, off = M.alloc("tmpf5", [128, D], F32, at=off); tmpf5.append(t_)
            t_, off = M.alloc("ot5", [128, D], F32, at=off); ot5.append(t_)
        p_w = pbase + 36
        hc_all = lambda tt: [("hcatT", k, tt) for k in range(16)]
        for tt in range(NT):
            i = tt % 2
            tsl = slice(tt * 128, (tt + 1) * 128)
            S.dma(xt5[i][:], x_d[b, tsl, :], writes=[("xt", i)])
            for half in range(2):
                bk = rot()
                for fc in range(16):
                    wv5 = pview(p_w + fc // 2, 2, 1024)
                    S.op("pe", lambda e, bk=bk, fc=fc, wv5=wv5, half=half, tsl=tsl: e.matmul(
                        ps[bk][:, 0:512], lhsT=hcatT[:, fc, tsl], rhs=wv5[:, fc % 2, half * 512:(half + 1) * 512], start=(fc == 0), stop=(fc == 15)),
                        [ptok(p_w + fc // 2), ("hcatT", fc, tt)], [("ps", bk)])
                S.op("act", lambda e, bk=bk, i=i, half=half: e.activation(out=tmpf5[i][:, half * 512:(half + 1) * 512], in_=ps[bk][:, 0:512], func=AF.Copy),
                     [("ps", bk)], [("tmpf", i, half)])
                S.op("dve", lambda e, i=i, half=half: e.tensor_tensor(out=tmpf5[i][:, half * 512:(half + 1) * 512], in0=tmpf5[i][:, half * 512:(half + 1) * 512],
                                                                      in1=gatet[:, half * 512:(half + 1) * 512], op=ALU.mult),
                     [("tmpf", i, half)] + [("gatet", q) for q in range(4)], [("tmpf", i, half)])
            S.op("pool", lambda e, i=i: e.tensor_tensor(out=tmpf5[i][:], in0=tmpf5[i][:], in1=xt5[i][:], op=ALU.add),
                 [("tmpf", i, 0), ("tmpf", i, 1), ("xt", i)], [("tmpf", i, 0), ("tmpf", i, 1)])
            S.op("act", lambda e, i=i, tt=tt: e.activation(out=ot5[i][:], in_=tmpf5[i][:], func=AF.Square, scale=1.0 / 32.0, accum_out=ssq[:, tt:tt + 1]),
                 [("tmpf", i, 0), ("tmpf", i, 1)], [("ot", i), ("ssq", tt)])
            S.op("pool", lambda e, tt=tt: e.tensor_scalar(out=rstd[:, tt:tt + 1], in0=ssq[:, tt:tt + 1], scalar1=EPS, scalar2=1.0, op0=ALU.add, op1=ALU.mult),
                 [("ssq", tt)], [("rstd", tt)])
            S.op("pool", lambda e, tt=tt: e.tensor_tensor(out=rstd[:, tt:tt + 1], in0=rstd[:, tt:tt + 1], in1=mhalf[:, 0:1], op=ALU.pow),
                 [("rstd", tt), "mhalf"], [("rstd", tt)])
            S.op("dve", lambda e, i=i, tt=tt: e.scalar_tensor_tensor(out=ot5[i][:], in0=tmpf5[i][:], scalar=rstd[:, tt:tt + 1], in1=fnw[:], op0=ALU.mult, op1=ALU.mult),
                 [("tmpf", i, 0), ("tmpf", i, 1), ("rstd", tt), "fnw"], [("ot", i)])
            S.dma(out_d[b, tsl, :], ot5[i][:], reads=[("ot", i)])
        for i in range(8):
            release(p_w + i)
    for b in range(NB):
        seq_body(b)
    S.emit(st)
    st.close()
    return nc


def _t5_bucket(n):
    n = np.maximum(n, 0)
    max_exact = 16
    nf = np.maximum(n, 1).astype(np.float32)
    large = max_exact + (np.log(nf / max_exact) / math.log(128 / max_exact) * (32 - max_exact)).astype(np.int32)
    large = np.minimum(large, 31)
    return np.where(n < max_exact, n, large)


def prep_inputs(inputs):
    f = lambda a: np.ascontiguousarray(np.asarray(a, dtype=np.float32))
    x = f(inputs["x"])
    c = f(inputs["c"])
    rep = lambda v, n=128: np.ascontiguousarray(np.broadcast_to(f(v).reshape(1, -1), (n, f(v).size)))
    consts = np.zeros((128, 4, 128), np.float32)
    consts[:, 0, :] = np.eye(128, dtype=np.float32)
    consts[:, 1, :] = np.triu(np.ones((128, 128), np.float32))
    consts[:, 2, :] = 1.0
    consts[:, 3, :] = np.where(np.triu(np.ones((128, 128), bool)), 0.0, NEG)
    cq = f(inputs["conv_q_w"])[0]
    ck = f(inputs["conv_k_w"])[0]
    convw = np.stack([cq, ck], 0).reshape(2, 4, KC, 128).transpose(3, 0, 2, 1)
    bif = np.concatenate([f(inputs["b_i"])[0], f(inputs["b_f"])[0]])
    bif = np.broadcast_to(bif.reshape(1, 1, 8), (128, NT, 8))
    lam = np.stack([f(inputs["lambda_q1"])[0], f(inputs["lambda_k1"])[0], f(inputs["lambda_q2"])[0], f(inputs["lambda_k2"])[0]], 0)
    lam = np.broadcast_to(lam.reshape(1, 4, 64), (128, 4, 64))
    rb = f(inputs["rel_bias"])
    kk = np.arange(128)[:, None]
    qq = np.arange(128)[None, :]
    idx0 = _t5_bucket(qq - kk)
    idx1 = _t5_bucket(128 + qq - kk)
    relb = np.stack([rb[idx0], rb[idx1]], 0)
    relb = relb.transpose(3, 1, 0, 2)
    c31 = np.broadcast_to(rb[31].reshape(1, 8), (128, 8))
    common = {
        "consts": consts,
        "normw": rep(inputs["norm_w"]),
        "fnw": rep(inputs["final_norm_w"]),
        "mnw": rep(inputs["m_norm_w"]),
        "anw": rep(inputs["a_norm_w"]),
        "convw": np.ascontiguousarray(convw),
        "bif": np.ascontiguousarray(bif),
        "lam": np.ascontiguousarray(lam),
        "relb": np.ascontiguousarray(relb),
        "c31": np.ascontiguousarray(c31),
        "bada": rep(inputs["b_ada"]),
        "w_ada": f(inputs["w_ada"])[0],
        "w_in": f(inputs["w_in"])[0],
        "w_out": f(inputs["w_out"])[0],
    }
    in_maps = []
    for i in range(N_CORES):
        m = dict(common)
        m["x"] = np.ascontiguousarray(x[i * NB:(i + 1) * NB])
        cs = c[i * NB:(i + 1) * NB]
        m["cT"] = np.ascontiguousarray(cs.reshape(NB, KC, 128).transpose(2, 1, 0))
        in_maps.append(m)
    return in_maps


def kernel(**inputs):
    in_maps = prep_inputs(inputs)
    nc = build_program()
    res = run_bass_kernel_spmd(nc, in_maps, core_ids=list(range(N_CORES)))
    out = np.concatenate([np.asarray(r["out"]) for r in res.results], axis=0)
    return out.astype(np.float32)
```

```python
import math
from contextlib import ExitStack

import numpy as np
import concourse.bass as bass
import concourse.mybir as mybir
from concourse.bass_utils import run_bass_kernel_spmd

F32 = mybir.dt.float32
BF16 = mybir.dt.bfloat16
ALU = mybir.AluOpType
AF = mybir.ActivationFunctionType

N_CORES = 8
D = 1024
S_LEN = 2048
NB = 2
NT = S_LEN // 128
KC = D // 128
MQ, MK, MV, MO, MZ, MI, AQ, AK, AV, AZ = 0, 1024, 2048, 3072, 4096, 5120, 5128, 6152, 7176, 8200
D_IN = 9224
EPS = 1e-6
WA = 260
WV = 132
NEG = -30000.0

COMPUTE = ("pe", "act", "dve", "pool")
ALLENG = COMPUTE + ("sp",)


class Op:
    __slots__ = ("eng", "fn", "deps", "signal", "seq", "dma", "sem", "val", "pv")

    def __init__(self, eng, fn, deps, dma):
        self.eng = eng
        self.fn = fn
        self.deps = deps
        self.signal = False
        self.seq = 0
        self.dma = dma
        self.sem = None
        self.val = 0
        self.pv = 0


class Sched:
    def __init__(self, nc, n_dma_sems=32):
        self.nc = nc
        self.ops = []
        self.tw = {}
        self.tr = {}
        self.eseq = {e: 0 for e in ALLENG}
        self.n_dma_sems = n_dma_sems

    SUBS = {}
    PH = "PHASE"

    def _expand(self, toks, ph):
        out = []
        for t in toks:
            out.append(t)
            if isinstance(t, tuple) and len(t) == 2 and t[0] == "ps" and t[1] in self.SUBS:
                out.extend(self.SUBS[t[1]])
        if ph and self.PH not in out:
            out.append(self.PH)
        return out

    cut = None
    marks = {}

    def mark(self, name):
        if name not in self.marks:
            self.marks[name] = len(self.ops)

    def op(self, eng, fn, reads=(), writes=(), dma=False, ph=True):
        idx = len(self.ops)
        if self.cut is not None and idx >= self.cut:
            return -1
        deps = set()
        tw, tr = self.tw, self.tr
        writes = self._expand(writes, False)
        reads = self._expand(reads, ph and (self.PH not in writes))
        for t in reads:
            w = tw.get(t)
            if w is not None:
                deps.add(w)
        for t in writes:
            w = tw.get(t)
            if w is not None:
                deps.add(w)
            r = tr.get(t)
            if r:
                deps.update(r)
        for t in reads:
            tr.setdefault(t, []).append(idx)
        for t in writes:
            tw[t] = idx
            tr[t] = []
        o = Op(eng, fn, deps, dma)
        self.eseq[eng] += 1
        o.seq = self.eseq[eng]
        self.ops.append(o)
        return idx

    def dma(self, out, in_, reads=(), writes=(), q="sp", ph=True):
        return self.op(q, lambda e, o=out, i=in_: e.dma_start(out=o, in_=i), reads, writes, dma=True, ph=ph)

    def emit(self, stack):
        nc = self.nc
        ops = self.ops
        for o in ops:
            keep = set()
            best = {}
            for d in o.deps:
                p = ops[d]
                if p.dma:
                    keep.add(d)
                    continue
                if p.eng == o.eng:
                    if o.eng == "pe" and not o.dma:
                        continue
                    if o.seq - p.seq > 2:
                        continue
                if p.eng not in best or best[p.eng][0] < p.seq:
                    best[p.eng] = (p.seq, d)
            for _e, (_sq, d) in best.items():
                ops[d].signal = True
                keep.add(d)
            o.deps = keep
        sems = {e: stack.enter_context(nc.semaphore("s_" + e)) for e in COMPUTE}
        dpools = {"sp": [stack.enter_context(nc.semaphore("dh%d" % i)) for i in range(self.n_dma_sems)],
                  "pool": [stack.enter_context(nc.semaphore("ds%d" % i)) for i in range(8)]}
        cnt = {e: 0 for e in COMPUTE}
        dcount = {}
        dk = {"sp": 0, "pool": 0}
        for o in ops:
            if o.dma:
                pool_ = dpools[o.eng]
                o.sem = pool_[dk[o.eng] % len(pool_)]
                dk[o.eng] += 1
                o.pv = dcount.get(id(o.sem), (None, 0))[1]
                o.val = o.pv + 16
                dcount[id(o.sem)] = (o.sem, o.val)
                o.signal = True
            else:
                if o.signal:
                    cnt[o.eng] += 1
                o.sem = sems[o.eng]
                o.val = cnt[o.eng]
        per = {e: [] for e in ALLENG}
        for o in ops:
            per[o.eng].append(o)
        block = stack.enter_context(nc.Block())

        def run(eng_name, engine):
            waited = {}
            for o in per[eng_name]:
                need = {}
                for d in o.deps:
                    p = ops[d]
                    key = id(p.sem)
                    if need.get(key, (None, 0))[1] < p.val:
                        need[key] = (p.sem, p.val)
                if o.dma and o.pv > 0:
                    key = id(o.sem)
                    if need.get(key, (None, 0))[1] < o.pv:
                        need[key] = (o.sem, o.pv)
                for key, (s, v) in need.items():
                    if waited.get(key, 0) < v:
                        engine.wait_ge(s, v)
                        waited[key] = v
                ins = o.fn(engine)
                if o.signal:
                    ins.then_inc(o.sem, 16 if o.dma else 1)
            if eng_name == "sp":
                for key, (sm_, v) in dcount.items():
                    if waited.get(key, 0) < v:
                        engine.wait_ge(sm_, v)

        @block.tensor
        def _(e):
            run("pe", e)

        @block.scalar
        def _(e):
            run("act", e)

        @block.vector
        def _(e):
            run("dve", e)

        @block.gpsimd
        def _(e):
            run("pool", e)

        @block.sync
        def _(e):
            run("sp", e)


class Mem:
    def __init__(self, nc, limit=224 * 1024 - 256):
        self.nc = nc
        self.off = 16 * 1024 + 256
        self.limit = limit
        self.n = 0
        self.work = None
        self.work_base = 0

    def alloc(self, name, shape, dt, at=None):
        esz = 2 if dt == BF16 else 4
        nel = int(np.prod(shape[1:]))
        nbytes = (nel * esz + 63) // 64 * 64
        if at is None:
            at = self.off
            self.off += nbytes
            assert self.off <= self.limit, (name, self.off)
            self.n += 1
            t = self.nc.alloc_sbuf_tensor_at("%s_%d" % (name, self.n), list(shape), dt, offset=at)
            return t, at + nbytes
        if self.work is None:
            self.work_base = self.off
            nwords = (self.limit - self.off) // 4
            self.work = self.nc.alloc_sbuf_tensor_at("work", [128, nwords], F32, offset=self.off)
        assert at + nbytes <= self.limit, (name, at + nbytes)
        w0 = (at - self.work_base) // 4
        v = self.work[:, w0:w0 + (nel * esz + 3) // 4]
        if dt == BF16:
            v = v.bitcast(BF16)[:, 0:nel]
        if len(shape) == 3:
            v = v.rearrange("p (a b) -> p a b", a=shape[1])
        elif len(shape) == 4:
            v = v.rearrange("p (a b c) -> p a b c", a=shape[1], b=shape[2])
        return v, at + nbytes


def build_program(stage=99, dumps=None, SUB=0, MUT=0):
    nc = bass.Bass("TRN2", target_bir_lowering=False)
    dr = lambda name, shape, dt=F32: nc.dram_tensor(name, list(shape), dt, kind="ExternalInput").ap()
    x_d = dr("x", [NB, S_LEN, D])
    cT_d = dr("cT", [128, KC, NB])
    consts_d = dr("consts", [128, 4, 128])
    normw_d = dr("normw", [128, D])
    fnw_d = dr("fnw", [128, D])
    mnw_d = dr("mnw", [128, D])
    anw_d = dr("anw", [128, 128])
    convw_d = dr("convw", [128, 2, KC, 4])
    bif_d = dr("bif", [128, NT, 8])
    lam_d = dr("lam", [128, 4, 64])
    relb_d = dr("relb", [8, 128, 2, 128])
    c31_d = dr("c31", [128, 8])
    bada_d = dr("bada", [128, 3 * D])
    wada_d = dr("w_ada", [D, 3 * D]).rearrange("(k p) j -> p k j", p=128)
    win_d = dr("w_in", [D, D_IN]).rearrange("(k p) j -> p k j", p=128)
    wout_d = dr("w_out", [2 * D, D]).rearrange("(k p) j -> p k j", p=128)
    out_d = nc.dram_tensor("out", [NB, S_LEN, D], F32, kind="ExternalOutput").ap()

    st = ExitStack()
    S = Sched(nc)
    M = Mem(nc)
    A = lambda name, shape, dt=F32: M.alloc(name, shape, dt)[0]

    def dump(name, ap, shape, reads, dt=F32):
        if dumps is None:
            return
        d = nc.dram_tensor("dbg_" + name, list(shape), dt, kind="ExternalOutput").ap()
        dumps[name] = (tuple(shape), dt)
        S.dma(d, ap, reads=reads)

    cst = A("cst", [128, 4, 128])
    identb = A("identb", [128, 128], BF16)
    mhalf = A("mhalf", [128, 16])
    zeros_f = A("zeros_f", [128, 16])
    normw = A("normw", [128, D])
    gt = A("gt", [128, D])
    sht = A("sht", [128, D])
    gatet = A("gatet", [128, D])
    fnw = A("fnw", [128, D])
    mnw = A("mnw", [128, D])
    anw = A("anw", [128, 128])
    convw = A("convw", [128, 2, KC, 4])
    bif = A("bif", [128, NT, 8])
    lamt = A("lamt", [128, 4, 64])
    lams = A("lams", [128, 16])
    c31 = A("c31", [128, 8])
    cT = A("cT", [128, KC, NB])
    cth = A("cth", [128, KC, NB])
    sc2 = A("sc2", [128, KC, NB])
    scB = A("scB", [128, KC, 128])
    wada = A("wada", [128, KC, 256])
    badap = A("badap", [128, 256])
    wg = A("wg", [128, KC, 8], BF16)
    gpre = A("gpre", [128, NT, 8])
    gef = A("gef", [128, NT, 4])
    glf = A("glf", [128, NT, 4])
    gtmp = A("gtmp", [128, NT, 4])
    g_u = A("g_u", [128, NT, 4])
    g_ebs = A("g_ebs", [128, NT, 4])
    g_ebL = A("g_ebL", [128, NT, 4])
    ssq = A("ssq", [128, NT])
    rstd = A("rstd", [128, NT])
    hT = A("hT", [128, KC, S_LEN], BF16)
    hcatT = A("hcatT", [128, 2 * KC, S_LEN], BF16)
    NSLOT = 8
    slots = [A("slot%d" % i, [128, 2048], BF16) for i in range(NSLOT)]
    lprod = A("lprod", [128, 2, 64])
    base = M.off

    ps = [st.enter_context(nc.psum_tensor("ps%d" % i, [128, 512], F32)) for i in range(8)]
    psb = [p[:].bitcast(BF16) for p in ps]

    ident_f = cst[:, 0, :]
    triu_f = cst[:, 1, :]
    ones_f = cst[:, 2, :]
    maskneg = cst[:, 3, :]

    S.dma(cst[:], consts_d, writes=["cst"])
    S.op("dve", lambda e: e.tensor_copy(out=identb[:], in_=ident_f), ["cst"], ["identb"])
    S.op("pool", lambda e: e.memset(mhalf[:], -0.5), [], ["mhalf"])
    S.op("pool", lambda e: e.memset(zeros_f[:], 0.0), [], ["zeros"])
    for t_sb, t_d, nm in ((normw, normw_d, "normw"), (fnw, fnw_d, "fnw"), (mnw, mnw_d, "mnw"), (anw, anw_d, "anw"),
                          (convw, convw_d, "convw"), (bif, bif_d, "bif"), (lamt, lam_d, "lamt"), (c31, c31_d, "c31"),
                          (cT, cT_d, "cT")):
        S.dma(t_sb[:], t_d, writes=[nm])
    S.dma(wg[:], win_d[:, :, MI:MI + 8], writes=["wg"], q="pool")
    S.op("act", lambda e: e.activation(out=cth[:], in_=cT[:], func=AF.Tanh, scale=0.5), ["cT"], ["cth"])
    S.op("dve", lambda e: e.scalar_tensor_tensor(out=sc2[:], in0=cth[:], scalar=1.0, in1=cT[:], op0=ALU.add, op1=ALU.mult),
         ["cth", "cT"], ["sc2"])
    lam_init = 0.8 - 0.6 * math.exp(-0.3 * 0)
    S.op("dve", lambda e: e.tensor_tensor(out=lprod[:, 0, :], in0=lamt[:, 0, :], in1=lamt[:, 1, :], op=ALU.mult), ["lamt"], ["lprod0"])
    S.op("dve", lambda e: e.tensor_tensor(out=lprod[:, 1, :], in0=lamt[:, 2, :], in1=lamt[:, 3, :], op=ALU.mult), ["lamt"], ["lprod1"])
    S.op("dve", lambda e: e.reduce_sum(out=lams[:, 0:1], in_=lprod[:, 0, :], axis=mybir.AxisListType.X), ["lprod0"], ["lams0"])
    S.op("dve", lambda e: e.reduce_sum(out=lams[:, 1:2], in_=lprod[:, 1, :], axis=mybir.AxisListType.X), ["lprod1"], ["lams1"])
    S.op("act", lambda e: e.activation(out=lams[:, 2:4], in_=lams[:, 0:2], func=AF.Exp), ["lams0", "lams1"], ["lams23"])
    S.op("dve", lambda e: e.tensor_tensor(out=lams[:, 4:5], in0=lams[:, 2:3], in1=lams[:, 3:4], op=ALU.subtract), ["lams23"], ["lams4"])
    S.op("dve", lambda e: e.tensor_scalar(out=lams[:, 5:6], in0=lams[:, 4:5], scalar1=lam_init, scalar2=-1.0, op0=ALU.add, op1=ALU.mult),
         ["lams4"], ["lams5"])
    neg_lam = lams[:, 5:6]
    dump("neglam", lams[:, 0:6], [128, 6], ["lams5"])

    ada_bank = [4]

    def adaln(b, which):
        S.op("dve", lambda e: e.tensor_copy(out=scB[:], in_=sc2[:, :, b:b + 1].to_broadcast([128, KC, 128])), ["sc2"], ["scB"])
        pieces = range(0, 8) if which == 0 else range(8, 12)
        for p in pieces:
            j0 = p * 256
            S.dma(wada[:], wada_d[:, :, j0:j0 + 256], writes=["wada"])
            S.dma(badap[:], bada_d[:, j0:j0 + 256], writes=["badap"])
            bk = ada_bank[0]
            ada_bank[0] = 4 + (bk - 4 + 1) % 2
            for kc in range(KC):
                S.op("pe", lambda e, kc=kc, bk=bk: e.matmul(ps[bk][:, 0:256], lhsT=scB[:, kc, :], rhs=wada[:, kc, :],
                                                           start=(kc == 0), stop=(kc == KC - 1)),
                     ["scB", "wada"], [("ps", bk)])
            if p < 4:
                dst = sht[:, j0:j0 + 256]
                S.op("dve", lambda e, bk=bk, dst=dst: e.scalar_tensor_tensor(out=dst, in0=ps[bk][:, 0:256], scalar=0.5, in1=badap[:],
                                                                              op0=ALU.mult, op1=ALU.add),
                     [("ps", bk), "badap"], [("sht", p)])
            elif p < 8:
                c0 = j0 - D
                dst = gt[:, c0:c0 + 256]
                S.op("dve", lambda e, bk=bk, dst=dst: e.scalar_tensor_tensor(out=dst, in0=ps[bk][:, 0:256], scalar=0.5, in1=badap[:],
                                                                              op0=ALU.mult, op1=ALU.add),
                     [("ps", bk), "badap"], [("gt", p - 4)])
                S.op("dve", lambda e, dst=dst, c0=c0: e.scalar_tensor_tensor(out=dst, in0=dst, scalar=1.0, in1=normw[:, c0:c0 + 256],
                                                                              op0=ALU.add, op1=ALU.mult),
                     [("gt", p - 4), "normw"], [("gt", p - 4)])
            else:
                c0 = j0 - 2 * D
                dst = gatet[:, c0:c0 + 256]
                S.op("dve", lambda e, bk=bk, dst=dst: e.scalar_tensor_tensor(out=dst, in0=ps[bk][:, 0:256], scalar=0.5, in1=badap[:],
                                                                              op0=ALU.mult, op1=ALU.add),
                     [("ps", bk), "badap"], [("gatet", p - 8)])

    pieces = []
    for _b in range(NB):
        for hd in range(4):
            for c0 in (MQ, MK, MV, MO, MZ):
                pieces.append((win_d[:, :, c0 + hd * 256:c0 + hd * 256 + 256], KC, 256))
        for pr in range(4):
            for c0 in (AQ, AK, AV, AZ):
                pieces.append((win_d[:, :, c0 + pr * 256:c0 + pr * 256 + 256], KC, 256))
        for i in range(8):
            pieces.append((wout_d[:, 2 * i:2 * i + 2, :], 2, 1024))
    PIECES_PER_SEQ = 20 + 16 + 8
    pstate = {"issued": 0}

    def issue_piece():
        i = pstate["issued"]
        if i >= len(pieces):
            return
        pstate["issued"] += 1
        src, k, n = pieces[i]
        dst = slots[i % NSLOT][:, 0:k * n].rearrange("p (k n) -> p k n", k=k)
        S.dma(dst, src, writes=[("slot", i % NSLOT)], q="pool", ph=False)

    def release(i):
        while pstate["issued"] <= i + NSLOT and pstate["issued"] < len(pieces):
            issue_piece()

    def pview(i, k, n):
        return slots[i % NSLOT][:, 0:k * n].rearrange("p (k n) -> p k n", k=k)

    def ptok(i):
        return ("slot", i % NSLOT)

    rot_state = {"n": 0}

    def rot():
        r = rot_state["n"] % 4
        rot_state["n"] += 1
        return r

    PH = "PHASE"

    def barrier():
        S.op("pool", lambda e: e.memset(mhalf[:, 8:16], -0.5), [], [PH])

    S.op("dve", lambda e: e.tensor_scalar(out=mnw[:], in0=mnw[:], scalar1=0.25, scalar2=None, op0=ALU.mult), ["mnw"], ["mnw4"])
    S.op("dve", lambda e: e.tensor_scalar(out=anw[:], in0=anw[:], scalar1=(1.0 - lam_init) * 0.5, scalar2=None, op0=ALU.mult), ["anw"], ["anw2"])
    for _ in range(NSLOT):
        issue_piece()

    def seq_body(b):
        if stage < 1:
            return
        adaln(b, 0)
        barrier()
        off = base
        xt = []
        tmpf = []
        hb = []
        for i in range(2):
            t_, off = M.alloc("xt", [128, D], F32, at=off); xt.append(t_)
            t_, off = M.alloc("tmpf", [128, D], F32, at=off); tmpf.append(t_)
            t_, off = M.alloc("hb", [128, D], BF16, at=off); hb.append(t_)
        for tt in range(NT):
            i = tt % 2
            S.dma(xt[i][:], x_d[b, tt * 128:(tt + 1) * 128, :], reads=[PH], writes=[("xt", i)])
            S.op("act", lambda e, i=i, tt=tt: e.activation(out=tmpf[i][:], in_=xt[i][:], func=AF.Square, scale=1.0 / 32.0,
                                                          accum_out=ssq[:, tt:tt + 1]),
                 [("xt", i), PH], [("tmpf", i), ("ssq", tt)])
            S.op("pool", lambda e, tt=tt: e.tensor_scalar(out=rstd[:, tt:tt + 1], in0=ssq[:, tt:tt + 1], scalar1=EPS, scalar2=1.0,
                                                          op0=ALU.add, op1=ALU.mult), [("ssq", tt)], [("rstd", tt)])
            S.op("pool", lambda e, tt=tt: e.tensor_tensor(out=rstd[:, tt:tt + 1], in0=rstd[:, tt:tt + 1], in1=mhalf[:, 0:1], op=ALU.pow),
                 [("rstd", tt), "mhalf"], [("rstd", tt)])
            S.op("dve", lambda e, i=i, tt=tt: e.scalar_tensor_tensor(out=tmpf[i][:], in0=xt[i][:], scalar=rstd[:, tt:tt + 1], in1=gt[:],
                                                                      op0=ALU.mult, op1=ALU.mult),
                 [("xt", i), ("rstd", tt), PH] + [("gt", q) for q in range(4)], [("tmpf", i)])
            S.op("pool", lambda e, i=i: e.tensor_tensor(out=hb[i][:], in0=tmpf[i][:], in1=sht[:], op=ALU.add),
                 [("tmpf", i), PH] + [("sht", q) for q in range(4)], [("hb", i)])
            bk = 6 + tt % 2
            for kc in range(KC):
                S.op("pe", lambda e, i=i, kc=kc, bk=bk: e.transpose(psb[bk][:, kc * 128:(kc + 1) * 128], hb[i][:, kc * 128:(kc + 1) * 128], identb[:]),
                     [("hb", i), "identb", PH], [("ps", bk)])
            S.op("act", lambda e, tt=tt, bk=bk: e.activation(out=hT[:, :, tt * 128:(tt + 1) * 128],
                                                            in_=psb[bk][:].rearrange("p (k t) -> p k t", k=KC), func=AF.Copy),
                 [("ps", bk)], [("hT", tt)])
        if b == 0:
            dump("gt", gt[:], [128, D], [("gt", q) for q in range(4)])
            dump("sht", sht[:], [128, D], [("sht", q) for q in range(4)])
            dump("hT", hT[:], [128, KC, S_LEN], [("hT", q) for q in range(NT)], BF16)
        if stage < 2:
            return
        hT_all = [("hT", q) for q in range(NT)]
        for tt in range(NT):
            for kc in range(KC):
                S.op("pe", lambda e, tt=tt, kc=kc: e.matmul(ps[2][:, tt * 8:(tt + 1) * 8], lhsT=hT[:, kc, tt * 128:(tt + 1) * 128],
                                                           rhs=wg[:, kc, :], start=(kc == 0), stop=(kc == KC - 1)),
                     [("hT", tt), "wg"], [("ps", 2)])
        S.op("dve", lambda e: e.tensor_tensor(out=gpre[:], in0=ps[2][:, 0:128].rearrange("p (t g) -> p t g", g=8), in1=bif[:], op=ALU.add),
             [("ps", 2), "bif"], ["gpre"])
        S.op("act", lambda e: e.activation(out=gef[:], in_=gpre[:, :, 4:8], func=AF.Exp, scale=-1.0), ["gpre"], ["gef"])
        S.op("act", lambda e: e.activation(out=glf[:], in_=gef[:], func=AF.Ln, bias=1.0), ["gef"], ["glf"])
        glf2 = glf[:].rearrange("p t g -> p (t g)")
        S.op("pe", lambda e: e.matmul(ps[3][:, 0:64], lhsT=triu_f, rhs=glf2, start=True, stop=True), ["glf", "cst"], [("ps", 3)])
        S.op("pe", lambda e: e.matmul(ps[3][:, 64:128], lhsT=ones_f, rhs=glf2, start=True, stop=True), ["glf", "cst"], [("ps", 3)])
        cum = ps[3][:, 0:64].rearrange("p (t g) -> p t g", g=4)
        tot = ps[3][:, 64:128].rearrange("p (t g) -> p t g", g=4)
        S.op("dve", lambda e: e.tensor_tensor(out=gtmp[:], in0=gpre[:, :, 0:4], in1=cum, op=ALU.add), ["gpre", ("ps", 3)], ["gtmp"])
        S.op("act", lambda e: e.activation(out=g_u[:], in_=gtmp[:], func=AF.Exp), ["gtmp"], ["g_u"])
        S.op("act", lambda e: e.activation(out=g_ebs[:], in_=cum, func=AF.Exp, scale=-1.0, bias=math.log(1.0 / 64.0)),
             [("ps", 3)], ["g_ebs"])
        S.op("act", lambda e: e.activation(out=g_ebL[:], in_=tot, func=AF.Exp, scale=-1.0), [("ps", 3)], ["g_ebL"])
        if b == 0:
            dump("g_u", g_u[:], [128, NT, 4], ["g_u"])
            dump("g_ebs", g_ebs[:], [128, NT, 4], ["g_ebs"])
            dump("g_ebL", g_ebL[:], [128, NT, 4], ["g_ebL"])
        if stage < 3:
            return
        pbase = b * PIECES_PER_SEQ
        def mhead(hd):
            barrier()
            off = base
            qT, off = M.alloc("qT", [128, 2, S_LEN], BF16, at=off)
            kT, off = M.alloc("kT", [128, 2, S_LEN], BF16, at=off)
            base2 = off
            qkT = (qT, kT)
            pre = []; acc = []; thb = []
            for i in range(2):
                t_, off = M.alloc("pre", [128, 515], F32, at=off); pre.append(t_)
                t_, off = M.alloc("acc", [128, 512], F32, at=off); acc.append(t_)
                t_, off = M.alloc("thb", [128, 512], F32, at=off); thb.append(t_)
            halo, off = M.alloc("halo", [128, 4, 4], F32, at=off)
            p_q = pbase + hd * 5
            n = 0
            for qk in range(2):
                pi_ = p_q + qk
                wv = pview(pi_, KC, 256)
                for dc in range(2):
                    dcg = hd * 2 + dc
                    for tg in range(4):
                        bk = rot()
                        pi = n % 2
                        n += 1
                        for kc in range(KC):
                            S.op("pe", lambda e, bk=bk, wv=wv, kc=kc, dc=dc, tg=tg: e.matmul(
                                ps[bk][:, 0:512], lhsT=wv[:, kc, dc * 128:(dc + 1) * 128], rhs=hT[:, kc, tg * 512:(tg + 1) * 512],
                                start=(kc == 0), stop=(kc == KC - 1)),
                                [ptok(pi_)] + hT_all[4 * tg:4 * tg + 4], [("ps", bk)])
                        S.op("act", lambda e, bk=bk, pi=pi: e.activation(out=pre[pi][:, 3:515], in_=ps[bk][:, 0:512], func=AF.Copy),
                             [("ps", bk), PH], [("pre", pi)])
                        hl = halo[:, qk * 2 + dc, 0:3]
                        if tg == 0:
                            S.op("pool", lambda e, pi=pi: e.memset(pre[pi][:, 0:3], 0.0), [PH], [("preh", pi)])
                        else:
                            S.op("pool", lambda e, pi=pi, hl=hl: e.tensor_copy(out=pre[pi][:, 0:3], in_=hl), [("halo", qk, dc), PH], [("preh", pi)])
                        S.op("pool", lambda e, pi=pi, hl=hl: e.tensor_copy(out=hl, in_=pre[pi][:, 512:515]), [("pre", pi), PH], [("halo", qk, dc)])
                        cw = lambda j, qk=qk, dcg=dcg: convw[:, qk, dcg, j:j + 1]
                        S.op("dve", lambda e, pi=pi, cw=cw: e.tensor_scalar(out=acc[pi][:], in0=pre[pi][:, 3:515], scalar1=cw(3), scalar2=None, op0=ALU.mult),
                             [("pre", pi), "convw", PH], [("acc", pi)])
                        for j in (2, 1, 0):
                            S.op("dve", lambda e, pi=pi, cw=cw, j=j: e.scalar_tensor_tensor(out=acc[pi][:], in0=pre[pi][:, j:j + 512], scalar=cw(j),
                                                                                         in1=acc[pi][:], op0=ALU.mult, op1=ALU.add),
                                 [("pre", pi), ("preh", pi), "convw", ("acc", pi), PH], [("acc", pi)])
                        S.op("act", lambda e, pi=pi: e.activation(out=thb[pi][:], in_=acc[pi][:], func=AF.Tanh, scale=0.5),
                             [("acc", pi), PH], [("thb", pi)])
                        S.op("pool", lambda e, pi=pi: e.tensor_scalar(out=thb[pi][:], in0=thb[pi][:], scalar1=1.0, scalar2=1.0, op0=ALU.add, op1=ALU.mult),
                             [("thb", pi), PH], [("thb", pi)])
                        dst = qkT[qk][:, dc, tg * 512:(tg + 1) * 512]
                        S.op("pool", lambda e, pi=pi, dst=dst: e.tensor_tensor(out=dst, in0=thb[pi][:], in1=acc[pi][:], op=ALU.mult),
                             [("thb", pi), ("acc", pi), PH], [("qkT", qk, dc, tg)])
                release(pi_)
            if b == 0 and hd == 0:
                dump("m_qT", qT[:], [128, 2, S_LEN], [("qkT", 0, dc, tg) for dc in range(2) for tg in range(4)], BF16)
                dump("m_kT", kT[:], [128, 2, S_LEN], [("qkT", 1, dc, tg) for dc in range(2) for tg in range(4)], BF16)
            if SUB == 1:
                return
            barrier()
            off = base2
            uv = []; Gp = []; kTok = []; maskS = []; Hn = []; hmb = []; sm = []; stats = []; mv = []
            for i in range(2):
                t_, off = M.alloc("uv", [128, WA], BF16, at=off); uv.append(t_)
                t_, off = M.alloc("Gp", [128, 256], F32, at=off); Gp.append(t_)
                t_, off = M.alloc("kTok", [128, 256], BF16, at=off); kTok.append(t_)
                t_, off = M.alloc("maskS", [128, 128], BF16, at=off); maskS.append(t_)
                t_, off = M.alloc("Hn", [128, 256], F32, at=off); Hn.append(t_)
                t_, off = M.alloc("hmb", [128, 256], BF16, at=off); hmb.append(t_)
                t_, off = M.alloc("sm", [128, 16], F32, at=off); sm.append(t_)
                t_, off = M.alloc("stats", [128, 6], F32, at=off); stats.append(t_)
                t_, off = M.alloc("mv", [128, 2], F32, at=off); mv.append(t_)
            to_, off = M.alloc("to", [128, 256], F32, at=off)
            tz_, off = M.alloc("tz", [128, 256], F32, at=off)
            zs_, off = M.alloc("zs", [128, 256], F32, at=off)
            Sf, off = M.alloc("Sf", [128, 128], F32, at=off)
            dCs, off = M.alloc("dCs", [128, 2, WA], F32, at=off)
            Mst, off = M.alloc("Mst", [128, 2, WA], F32, at=off)
            Cb, off = M.alloc("Cb", [128, 2, WA], BF16, at=off)
            assert off <= M.limit, off
            p_v, p_o, p_z = p_q + 2, p_q + 3, p_q + 4
            wvv, wvo, wvz = pview(p_v, KC, 256), pview(p_o, KC, 256), pview(p_z, KC, 256)
            mn4 = mnw[:, hd * 256:(hd + 1) * 256]
            for i in range(2):
                S.op("dve", lambda e, i=i: e.tensor_copy(out=uv[i][:, 256:WA], in_=zeros_f[:, 0:WA - 256]), ["zeros"], [("uv1", i)])
            for c in range(NT):
                ci = c % 2
                tsl = slice(c * 128, (c + 1) * 128)
                bA, bB = rot(), rot()
                for (wv_, pt_, bk, c0) in ((wvv, p_v, bA, 0), (wvo, p_o, bA, 256), (wvz, p_z, bB, 0)):
                    for kc in range(KC):
                        S.op("pe", lambda e, wv_=wv_, bk=bk, c0=c0, kc=kc, tsl=tsl: e.matmul(
                            ps[bk][:, c0:c0 + 256], lhsT=hT[:, kc, tsl], rhs=wv_[:, kc, :], start=(kc == 0), stop=(kc == KC - 1)),
                            [ptok(pt_), ("hT", c)], [("ps", bk)])
                u_c = g_u[:, c, hd:hd + 1]
                S.op("act", lambda e, ci=ci, bA=bA, u_c=u_c: e.activation(out=uv[ci][:, 0:256], in_=ps[bA][:, 0:256], func=AF.Copy, scale=u_c),
                     [("ps", bA), "g_u", PH], [("uv", ci)])
                S.op("act", lambda e, ci=ci, u_c=u_c: e.activation(out=uv[ci][:, 256:257], in_=u_c, func=AF.Copy), ["g_u", PH], [("uv1", ci)])
                S.op("act", lambda e, bA=bA: e.activation(out=to_[:], in_=ps[bA][:, 256:512], func=AF.Tanh, scale=0.5), [("ps", bA), PH], ["to"])
                S.op("act", lambda e, bB=bB: e.activation(out=tz_[:], in_=ps[bB][:, 0:256], func=AF.Tanh, scale=0.5), [("ps", bB), PH], ["tz"])
                S.op("act", lambda e, bB=bB: e.activation(out=zs_[:], in_=ps[bB][:, 0:256], func=AF.Copy), [("ps", bB), PH], ["zs"])
                S.op("dve", lambda e, ci=ci: e.scalar_tensor_tensor(out=Gp[ci][:], in0=to_[:], scalar=1.0, in1=zs_[:], op0=ALU.add, op1=ALU.mult),
                     ["to", "zs", PH], [("Gp", ci)])
                S.op("dve", lambda e, ci=ci: e.scalar_tensor_tensor(out=Gp[ci][:], in0=tz_[:], scalar=1.0, in1=Gp[ci][:], op0=ALU.add, op1=ALU.mult),
                     ["tz", ("Gp", ci), PH], [("Gp", ci)])
                S.op("pool", lambda e, ci=ci: e.tensor_tensor(out=Gp[ci][:], in0=Gp[ci][:], in1=mn4, op=ALU.mult), [("Gp", ci), "mnw4", PH], [("Gp", ci)])
                S.mark("A%d" % c)
                for dc in range(2):
                    S.op("pe", lambda e, dc=dc, tsl=tsl: e.transpose(psb[4][:, dc * 128:(dc + 1) * 128], kT[:, dc, tsl], identb[:]),
                         [("qkT", 1, dc, c // 4), "identb"], [("ps", 4)])
                S.op("act", lambda e, ci=ci: e.activation(out=kTok[ci][:], in_=psb[4][:, 0:256], func=AF.Copy), [("ps", 4), PH], [("kTok", ci)])
                for dc in range(2):
                    S.op("pe", lambda e, dc=dc, tsl=tsl: e.matmul(ps[5][:, 0:128], lhsT=kT[:, dc, tsl], rhs=qT[:, dc, tsl], start=(dc == 0), stop=(dc == 1)),
                         [("qkT", 1, dc, c // 4), ("qkT", 0, dc, c // 4)], [("ps", 5)])
                S.op("act", lambda e: e.activation(out=Sf[:], in_=ps[5][:, 0:128], func=AF.Copy), [("ps", 5), PH], ["Sf"])
                S.op("pool", lambda e, ci=ci: e.tensor_tensor(out=maskS[ci][:], in0=Sf[:], in1=triu_f, op=ALU.mult),
                     ["Sf", "cst", PH], [("maskS", ci)])
                S.mark("B%d" % c)
                if c > 0:
                    for dc in range(2):
                        S.op("pe", lambda e, dc=dc, tsl=tsl: e.matmul(ps[6][:, 0:WA], lhsT=qT[:, dc, tsl], rhs=Cb[:, dc, :], start=(dc == 0), stop=False),
                             [("qkT", 0, dc, c // 4), "Cb"], [("ps", 6)])
                S.op("pe", lambda e, ci=ci, c=c: e.matmul(ps[6][:, 0:WA], lhsT=maskS[ci][:], rhs=uv[ci][:], start=(c == 0), stop=True),
                     [("maskS", ci), ("uv", ci), ("uv1", ci)], [("ps", 6)])
                ebs_c = g_ebs[:, c, hd:hd + 1]
                smc = sm[ci]
                S.op("act", lambda e, smc=smc: e.activation(out=smc[:, 7:8], in_=ps[6][:, 256:257], func=AF.Copy), [("ps", 6), PH], [("sm", ci, 7)])
                S.op("dve", lambda e, smc=smc, ebs_c=ebs_c: e.tensor_tensor(out=smc[:, 0:1], in0=smc[:, 7:8], in1=ebs_c, op=ALU.mult),
                     [("sm", ci, 7), "g_ebs", PH], [("sm", ci, 0)])
                S.op("dve", lambda e, smc=smc: e.tensor_tensor(out=smc[:, 1:2], in0=smc[:, 0:1], in1=smc[:, 0:1], op=ALU.mult), [("sm", ci, 0), PH], [("sm", ci, 1)])
                S.op("dve", lambda e, smc=smc: e.tensor_scalar(out=smc[:, 2:3], in0=smc[:, 1:2], scalar1=1.0, scalar2=None, op0=ALU.max), [("sm", ci, 1), PH], [("sm", ci, 2)])
                S.op("pool", lambda e, smc=smc: e.tensor_tensor(out=smc[:, 3:4], in0=smc[:, 2:3], in1=mhalf[:, 0:1], op=ALU.pow), [("sm", ci, 2), "mhalf", PH], [("sm", ci, 3)])
                S.op("dve", lambda e, smc=smc, ebs_c=ebs_c: e.tensor_tensor(out=smc[:, 4:5], in0=smc[:, 3:4], in1=ebs_c, op=ALU.mult), [("sm", ci, 3), "g_ebs", PH], [("sm", ci, 4)])
                S.op("act", lambda e, ci=ci, smc=smc: e.activation(out=Hn[ci][:], in_=ps[6][:, 0:256], func=AF.Copy, scale=smc[:, 4:5]),
                     [("ps", 6), ("sm", ci, 4), PH], [("Hn", ci)])
                S.mark("C%d" % c)
                S.op("dve", lambda e, ci=ci: e.bn_stats(out=stats[ci][:], in_=Hn[ci][:]), [("Hn", ci), PH], [("stats", ci)])
                S.op("dve", lambda e, ci=ci: e.bn_aggr(out=mv[ci][:], in_=stats[ci][:]), [("stats", ci), PH], [("mv", ci)])
                S.op("pool", lambda e, ci=ci, smc=smc: e.tensor_scalar(out=smc[:, 5:6], in0=mv[ci][:, 1:2], scalar1=EPS, scalar2=1.0, op0=ALU.add, op1=ALU.mult),
                     [("mv", ci), PH], [("sm", ci, 5)])
                S.op("pool", lambda e, smc=smc: e.tensor_tensor(out=smc[:, 6:7], in0=smc[:, 5:6], in1=mhalf[:, 0:1], op=ALU.pow), [("sm", ci, 5), "mhalf", PH], [("sm", ci, 6)])
                S.op("dve", lambda e, ci=ci, smc=smc: e.tensor_scalar(out=Hn[ci][:], in0=Hn[ci][:], scalar1=mv[ci][:, 0:1], scalar2=smc[:, 6:7],
                                                                      op0=ALU.subtract, op1=ALU.mult),
                     [("Hn", ci), ("mv", ci), ("sm", ci, 6), PH], [("Hn", ci)])
                S.op("pool", lambda e, ci=ci: e.tensor_tensor(out=hmb[ci][:], in0=Hn[ci][:], in1=Gp[ci][:], op=ALU.mult), [("Hn", ci), ("Gp", ci), PH], [("hmb", ci)])
                bT = rot()
                for dc in range(2):
                    S.op("pe", lambda e, ci=ci, dc=dc, bT=bT: e.transpose(psb[bT][:, dc * 128:(dc + 1) * 128], hmb[ci][:, dc * 128:(dc + 1) * 128], identb[:]),
                         [("hmb", ci), "identb"], [("ps", bT)])
                S.op("act", lambda e, tsl=tsl, bT=bT: e.activation(out=hcatT[:, hd * 2:hd * 2 + 2, tsl], in_=psb[bT][:, 0:256].rearrange("p (k t) -> p k t", k=2), func=AF.Copy),
                     [("ps", bT)], [("hcatT", hd * 2, c), ("hcatT", hd * 2 + 1, c)])
                S.mark("D%d" % c)
                if c < NT - 1:
                    S.op("pe", lambda e, ci=ci: e.matmul(ps[7][:, 0:WA], lhsT=kTok[ci][:, 0:128], rhs=uv[ci][:], start=True, stop=True),
                         [("kTok", ci), ("uv", ci), ("uv1", ci)], [("ps", 7)])
                    bD = rot()
                    S.op("pe", lambda e, ci=ci, bD=bD: e.matmul(ps[bD][:, 0:WA], lhsT=kTok[ci][:, 128:256], rhs=uv[ci][:], start=True, stop=True),
                         [("kTok", ci), ("uv", ci), ("uv1", ci)], [("ps", bD)])
                    for dc, (src, tk) in enumerate(((ps[7][:, 0:WA], ("ps", 7)), (ps[bD][:, 0:WA], ("ps", bD)))):
                        S.op("act", lambda e, dc=dc, src=src: e.activation(out=dCs[:, dc, :], in_=src, func=AF.Copy), [tk, PH], [("dCs", dc)])
                        if c == 0:
                            S.op("dve", lambda e, dc=dc: e.tensor_copy(out=Mst[:, dc, :], in_=dCs[:, dc, :]), [("dCs", dc), PH, "Cb"], [("Mst", dc)])
                        else:
                            ebp = g_ebL[:, c - 1, hd:hd + 1]
                            S.op("dve", lambda e, dc=dc, ebp=ebp: e.scalar_tensor_tensor(out=Mst[:, dc, :], in0=Mst[:, dc, :], scalar=ebp, in1=dCs[:, dc, :],
                                                                                         op0=ALU.mult, op1=ALU.add),
                                 [("dCs", dc), ("Mst", dc), "g_ebL", PH], [("Mst", dc)])
                    ebc = g_ebL[:, c, hd:hd + 1]
                    S.op("act", lambda e, ebc=ebc: e.activation(out=Cb[:], in_=Mst[:], func=AF.Copy, scale=ebc), [("Mst", 0), ("Mst", 1), "g_ebL", PH], ["Cb"])
                S.mark("E%d" % c)
            release(p_v); release(p_o); release(p_z)

        for hd in range(4 if stage >= 4 else 1):
            mhead(hd)
        if b == 0 and SUB != 1:
            nhd = 8 if stage >= 4 else 2
            dump("hcat_m", hcatT[:, 0:nhd, :], [128, nhd, S_LEN], [("hcatT", k, c) for k in range(nhd) for c in range(NT)], BF16)
        if stage < 5:
            return
        def apair(pr):
            p_aq = pbase + 20 + pr * 4
            p_ak, p_av, p_az = p_aq + 1, p_aq + 2, p_aq + 3
            def ahead(hh):
                ah = pr * 2 + hh
                barrier()
                off = base
                qTa, off = M.alloc("qTa", [128, S_LEN], BF16, at=off)
                kTa, off = M.alloc("kTa", [128, S_LEN], BF16, at=off)
                Vaug, off = M.alloc("Vaug", [128, NT, WV], BF16, at=off)
                Gz = []; Pb = []; tmpS = []; tza = []; t1 = []; t0 = []; zsa = []; ha = []; hab = []; sma = []
                for i in range(2):
                    t_, off = M.alloc("Gz", [128, 4, 128], F32, at=off); Gz.append(t_)
                    t_, off = M.alloc("tmpS", [128, 256], F32, at=off); tmpS.append(t_)
                    t_, off = M.alloc("tza", [128, 128], F32, at=off); tza.append(t_)
                    t_, off = M.alloc("t1", [128, 128], F32, at=off); t1.append(t_)
                    t_, off = M.alloc("t0", [128, 128], F32, at=off); t0.append(t_)
                    t_, off = M.alloc("zsa", [128, 128], F32, at=off); zsa.append(t_)
                    t_, off = M.alloc("ha", [128, 128], F32, at=off); ha.append(t_)
                    t_, off = M.alloc("hab", [128, 128], BF16, at=off); hab.append(t_)
                    t_, off = M.alloc("sma", [128, 16], F32, at=off); sma.append(t_)
                for i in range(3):
                    t_, off = M.alloc("Pb", [128, 512], BF16, at=off); Pb.append(t_)
                relb, off = M.alloc("relb", [128, 2, 128], F32, at=off)
                junk, off = M.alloc("junk", [128, 128], F32, at=off)
                assert off <= M.limit, off
                csl = slice(hh * 128, (hh + 1) * 128)
                wq, wk, wv_, wz = pview(p_aq, KC, 256), pview(p_ak, KC, 256), pview(p_av, KC, 256), pview(p_az, KC, 256)
                S.dma(relb[:], relb_d[ah], writes=["relb"])
                S.op("dve", lambda e: e.tensor_tensor(out=relb[:, 0, :], in0=relb[:, 0, :], in1=maskneg, op=ALU.add), ["relb", "cst"], ["relb"])
                S.op("act", lambda e: e.activation(out=relb[:], in_=relb[:], func=AF.Exp), ["relb"], ["relb"])
                S.op("dve", lambda e: e.tensor_copy(out=Vaug[:, :, 128:WV], in_=ones_f[:, 0:NT * (WV - 128)].rearrange("p (a b) -> p a b", a=NT)), ["cst"], ["Vones"])
                for isk, (w_, pt_, dstT) in enumerate(((wq, p_aq, qTa), (wk, p_ak, kTa))):
                    for tg in range(4):
                        bk = rot()
                        for kc in range(KC):
                            S.op("pe", lambda e, bk=bk, w_=w_, kc=kc, tg=tg: e.matmul(
                                ps[bk][:, 0:512], lhsT=w_[:, kc, csl], rhs=hT[:, kc, tg * 512:(tg + 1) * 512], start=(kc == 0), stop=(kc == KC - 1)),
                                [ptok(pt_)] + hT_all[4 * tg:4 * tg + 4], [("ps", bk)])
                        dst = dstT[:, tg * 512:(tg + 1) * 512]
                        if isk == 0:
                            S.op("act", lambda e, bk=bk, dst=dst: e.activation(out=dst, in_=ps[bk][:, 0:512], func=AF.Copy, scale=0.125),
                                 [("ps", bk)], [("qTa", tg)])
                        else:
                            S.op("act", lambda e, bk=bk, dst=dst: e.activation(out=dst, in_=ps[bk][:, 0:512], func=AF.Copy), [("ps", bk)], [("kTa", tg)])
                c31h = c31[:, ah:ah + 1]
                pst = {"p": 0, "s": 0}
                S.mark("AA")
                for G in range(4):
                    gi = G % 2
                    for tt in range(4 * G, 4 * G + 4):
                        bk = rot()
                        tsl = slice(tt * 128, (tt + 1) * 128)
                        for (w_, pt_, c0) in ((wv_, p_av, 0), (wz, p_az, 128)):
                            for kc in range(KC):
                                S.op("pe", lambda e, bk=bk, w_=w_, kc=kc, c0=c0, tsl=tsl: e.matmul(
                                    ps[bk][:, c0:c0 + 128], lhsT=hT[:, kc, tsl], rhs=w_[:, kc, csl], start=(kc == 0), stop=(kc == KC - 1)),
                                    [ptok(pt_), ("hT", tt)], [("ps", bk)])
                        S.op("act", lambda e, bk=bk, tt=tt: e.activation(out=Vaug[:, tt, 0:128], in_=ps[bk][:, 0:128], func=AF.Copy), [("ps", bk)], [("Vaug", tt)])
                        ti = tt % 2
                        S.op("act", lambda e, bk=bk, ti=ti: e.activation(out=tza[ti][:], in_=ps[bk][:, 128:256], func=AF.Tanh, scale=0.5), [("ps", bk)], [("tza", ti)])
                        gdst = Gz[gi][:, tt % 4, :]
                        S.op("act", lambda e, bk=bk, ti=ti: e.activation(out=zsa[ti][:], in_=ps[bk][:, 128:256], func=AF.Copy), [("ps", bk)], [("zsa", ti)])
                        S.op("dve", lambda e, ti=ti, gdst=gdst: e.scalar_tensor_tensor(out=gdst, in0=tza[ti][:], scalar=1.0, in1=zsa[ti][:], op0=ALU.add, op1=ALU.mult),
                             [("tza", ti), ("zsa", ti)], [("Gz", gi, tt % 4)])
                        S.op("pool", lambda e, gdst=gdst: e.tensor_tensor(out=gdst, in0=gdst, in1=anw[:], op=ALU.mult), [("Gz", gi, tt % 4), "anw2"], [("Gz", gi, tt % 4)])
                    S.mark("AB")
                    steps = [(m, j) for m in range(2) for j in range(4 * G + 4)]

                    def emit_scores(m, j):
                        msl = slice(m * 64, (m + 1) * 64)
                        qlo = max(j, 4 * G)
                        nq = 4 * G + 4 - qlo
                        N = nq * 128
                        bk = rot()
                        S.op("pe", lambda e, bk=bk, msl=msl, j=j, qlo=qlo, N=N: e.matmul(
                            ps[bk][:, 0:N], lhsT=kTa[msl, j * 128:(j + 1) * 128], rhs=qTa[msl, qlo * 128:qlo * 128 + N], start=True, stop=True),
                            [("kTa", j // 4)] + [("qTa", G)], [("ps", bk)])
                        pi = pst["p"] % 3
                        pst["p"] += 1
                        nnear = max(0, min(nq, j + 2 - qlo))
                        if nnear > 0:
                            ty0 = qlo - j
                            si = pst["s"] % 2
                            pst["s"] += 1
                            tsv = tmpS[si]
                            S.op("act", lambda e, bk=bk, tsv=tsv, nnear=nnear: e.activation(out=tsv[:, 0:nnear * 128], in_=ps[bk][:, 0:nnear * 128], func=AF.Exp),
                                 [("ps", bk)], [("tmpS", si)])
                            S.op("dve", lambda e, pi=pi, tsv=tsv, nnear=nnear, ty0=ty0: e.tensor_tensor(
                                out=Pb[pi][:, 0:nnear * 128], in0=tsv[:, 0:nnear * 128],
                                in1=relb[:, ty0:ty0 + nnear, :].rearrange("p a q -> p (a q)"), op=ALU.mult),
                                [("tmpS", si), "relb"], [("P", pi, "n")])
                        if nq > nnear:
                            S.op("act", lambda e, pi=pi, bk=bk, nnear=nnear, N=N: e.activation(out=Pb[pi][:, nnear * 128:N], in_=ps[bk][:, nnear * 128:N],
                                                                                            func=AF.Exp, bias=c31h),
                                 [("ps", bk), "c31"], [("P", pi, "f")])
                        return (pi, qlo, nq)

                    def emit_pv(m, j, info):
                        pi, qlo, nq = info
                        for qi in range(nq):
                            qb = qlo + qi
                            ql = qb - 4 * G
                            abk = 4 + m * 2 + ql // 2
                            col0 = (ql % 2) * 256
                            S.op("pe", lambda e, pi=pi, qi=qi, abk=abk, col0=col0, j=j, qb=qb, ql=ql: e.matmul(
                                ps[abk][:, col0:col0 + WV], lhsT=Pb[pi][:, qi * 128:(qi + 1) * 128], rhs=Vaug[:, j, :],
                                start=(j == 0 and ql % 2 == 0), stop=(j == qb and ql % 2 == 1)),
                                [("P", pi, "n"), ("P", pi, "f"), ("Vaug", j), "Vones"], [("ps", abk)])

                    info = emit_scores(*steps[0])
                    for si_, (m, j) in enumerate(steps):
                        nxt = emit_scores(*steps[si_ + 1]) if si_ + 1 < len(steps) else None
                        emit_pv(m, j, info)
                        info = nxt
                    S.mark("AC")
                    for ql in range(4):
                        qb = 4 * G + ql
                        fi = ql % 2
                        a0 = ps[4 + ql // 2][:, (ql % 2) * 256:(ql % 2) * 256 + 129]
                        a1 = ps[6 + ql // 2][:, (ql % 2) * 256:(ql % 2) * 256 + 129]
                        tk0 = ("ps", 4 + ql // 2)
                        tk1 = ("ps", 6 + ql // 2)
                        sm_ = sma[fi]
                        S.op("act", lambda e, sm_=sm_, a0=a0: e.activation(out=sm_[:, 6:7], in_=a0[:, 128:129], func=AF.Copy), [tk0], [("sma", fi, 6)])
                        S.op("act", lambda e, sm_=sm_, a1=a1: e.activation(out=sm_[:, 7:8], in_=a1[:, 128:129], func=AF.Copy), [tk1], [("sma", fi, 7)])
                        S.op("dve", lambda e, sm_=sm_: e.reciprocal(out=sm_[:, 0:1], in_=sm_[:, 6:7]), [("sma", fi, 6)], [("sma", fi, 0)])
                        S.op("dve", lambda e, sm_=sm_: e.reciprocal(out=sm_[:, 1:2], in_=sm_[:, 7:8]), [("sma", fi, 7)], [("sma", fi, 1)])
                        S.op("dve", lambda e, sm_=sm_: e.tensor_tensor(out=sm_[:, 2:3], in0=sm_[:, 1:2], in1=neg_lam, op=ALU.mult), [("sma", fi, 1), "lams5"], [("sma", fi, 2)])
                        S.op("act", lambda e, fi=fi, sm_=sm_, a1=a1: e.activation(out=t1[fi][:], in_=a1[:, 0:128], func=AF.Copy, scale=sm_[:, 2:3]),
                             [tk1, ("sma", fi, 2)], [("t1", fi)])
                        S.op("act", lambda e, fi=fi, sm_=sm_, a0=a0: e.activation(out=t0[fi][:], in_=a0[:, 0:128], func=AF.Copy, scale=sm_[:, 0:1]),
                             [tk0, ("sma", fi, 0)], [("t0", fi)])
                        S.op("dve", lambda e, fi=fi: e.tensor_tensor(out=ha[fi][:], in0=t0[fi][:], in1=t1[fi][:], op=ALU.add),
                             [("t0", fi), ("t1", fi)], [("ha", fi)])
                        S.op("act", lambda e, fi=fi, sm_=sm_: e.activation(out=junk[:], in_=ha[fi][:], func=AF.Square, scale=128.0 ** -0.5, accum_out=sm_[:, 3:4]),
                             [("ha", fi)], ["junk", ("sma", fi, 3)])
                        S.op("pool", lambda e, sm_=sm_: e.tensor_scalar(out=sm_[:, 4:5], in0=sm_[:, 3:4], scalar1=EPS, scalar2=1.0, op0=ALU.add, op1=ALU.mult),
                             [("sma", fi, 3)], [("sma", fi, 4)])
                        S.op("pool", lambda e, sm_=sm_: e.tensor_tensor(out=sm_[:, 5:6], in0=sm_[:, 4:5], in1=mhalf[:, 0:1], op=ALU.pow), [("sma", fi, 4), "mhalf"], [("sma", fi, 5)])
                        S.op("dve", lambda e, fi=fi, sm_=sm_, ql=ql, gi=gi: e.scalar_tensor_tensor(out=hab[fi][:], in0=ha[fi][:], scalar=sm_[:, 5:6], in1=Gz[gi][:, ql, :],
                                                                                             op0=ALU.mult, op1=ALU.mult),
                             [("ha", fi), ("sma", fi, 5), ("Gz", gi, ql)], [("hab", fi)])
                        bk = rot()
                        S.op("pe", lambda e, fi=fi, bk=bk: e.transpose(psb[bk][:, 0:128], hab[fi][:], identb[:]), [("hab", fi), "identb"], [("ps", bk)])
                        S.op("act", lambda e, bk=bk, qb=qb: e.activation(out=hcatT[:, 8 + ah, qb * 128:(qb + 1) * 128], in_=psb[bk][:, 0:128], func=AF.Copy),
                             [("ps", bk)], [("hcatT", 8 + ah, qb)])
            for hh in range(2):
                ahead(hh)
                S.mark("AD")
            release(p_aq); release(p_ak); release(p_av); release(p_az)

        for pr in range(4 if stage >= 6 else 1):
            apair(pr)
        if b == 0:
            nha = 8 if stage >= 6 else 2
            dump("hcat_a", hcatT[:, 8:8 + nha, :], [128, nha, S_LEN], [("hcatT", k, c) for k in range(8, 8 + nha) for c in range(NT)], BF16)
        if stage < 7:
            return
        adaln(b, 1)
        barrier()
        off = base
        xt5 = []; tmpf5 = []; ot5 = []
        for i in range(2):
            t_, off = M.alloc("xt5", [128, D], F32, at=off); xt5.append(t_)
            t_, off = M.alloc("tmpf5", [128, D], F32, at=off); tmpf5.append(t_)
            t_, off = M.alloc("ot5", [128, D], F32, at=off); ot5.append(t_)
        p_w = pbase + 36
        hc_all = lambda tt: [("hcatT", k, tt) for k in range(16)]
        for tt in range(NT):
            i = tt % 2
            tsl = slice(tt * 128, (tt + 1) * 128)
            S.dma(xt5[i][:], x_d[b, tsl, :], writes=[("xt", i)])
            for half in range(2):
                bk = rot()
                for fc in range(16):
                    wv5 = pview(p_w + fc // 2, 2, 1024)
                    S.op("pe", lambda e, bk=bk, fc=fc, wv5=wv5, half=half, tsl=tsl: e.matmul(
                        ps[bk][:, 0:512], lhsT=hcatT[:, fc, tsl], rhs=wv5[:, fc % 2, half * 512:(half + 1) * 512], start=(fc == 0), stop=(fc == 15)),
                        [ptok(p_w + fc // 2), ("hcatT", fc, tt)], [("ps", bk)])
                S.op("act", lambda e, bk=bk, i=i, half=half: e.activation(out=tmpf5[i][:, half * 512:(half + 1) * 512], in_=ps[bk][:, 0:512], func=AF.Copy),
                     [("ps", bk)], [("tmpf", i, half)])
                S.op("dve", lambda e, i=i, half=half: e.tensor_tensor(out=tmpf5[i][:, half * 512:(half + 1) * 512], in0=tmpf5[i][:, half * 512:(half + 1) * 512],
                                                                      in1=gatet[:, half * 512:(half + 1) * 512], op=ALU.mult),
                     [("tmpf", i, half)] + [("gatet", q) for q in range(4)], [("tmpf", i, half)])
            S.op("pool", lambda e, i=i: e.tensor_tensor(out=tmpf5[i][:], in0=tmpf5[i][:], in1=xt5[i][:], op=ALU.add),
                 [("tmpf", i, 0), ("tmpf", i, 1), ("xt", i)], [("tmpf", i, 0), ("tmpf", i, 1)])
            S.op("act", lambda e, i=i, tt=tt: e.activation(out=ot5[i][:], in_=tmpf5[i][:], func=AF.Square, scale=1.0 / 32.0, accum_out=ssq[:, tt:tt + 1]),
                 [("tmpf", i, 0), ("tmpf", i, 1)], [("ot", i), ("ssq", tt)])
            S.op("pool", lambda e, tt=tt: e.tensor_scalar(out=rstd[:, tt:tt + 1], in0=ssq[:, tt:tt + 1], scalar1=EPS, scalar2=1.0, op0=ALU.add, op1=ALU.mult),
                 [("ssq", tt)], [("rstd", tt)])
            S.op("pool", lambda e, tt=tt: e.tensor_tensor(out=rstd[:, tt:tt + 1], in0=rstd[:, tt:tt + 1], in1=mhalf[:, 0:1], op=ALU.pow),
                 [("rstd", tt), "mhalf"], [("rstd", tt)])
            S.op("dve", lambda e, i=i, tt=tt: e.scalar_tensor_tensor(out=ot5[i][:], in0=tmpf5[i][:], scalar=rstd[:, tt:tt + 1], in1=fnw[:], op0=ALU.mult, op1=ALU.mult),
                 [("tmpf", i, 0), ("tmpf", i, 1), ("rstd", tt), "fnw"], [("ot", i)])
            S.dma(out_d[b, tsl, :], ot5[i][:], reads=[("ot", i)])
        for i in range(8):
            release(p_w + i)
    for b in range(NB):
        seq_body(b)
    S.emit(st)
    st.close()
    return nc


def _t5_bucket(n):
    n = np.maximum(n, 0)
    max_exact = 16
    nf = np.maximum(n, 1).astype(np.float32)
    large = max_exact + (np.log(nf / max_exact) / math.log(128 / max_exact) * (32 - max_exact)).astype(np.int32)
    large = np.minimum(large, 31)
    return np.where(n < max_exact, n, large)


def prep_inputs(inputs):
    f = lambda a: np.ascontiguousarray(np.asarray(a, dtype=np.float32))
    x = f(inputs["x"])
    c = f(inputs["c"])
    rep = lambda v, n=128: np.ascontiguousarray(np.broadcast_to(f(v).reshape(1, -1), (n, f(v).size)))
    consts = np.zeros((128, 4, 128), np.float32)
    consts[:, 0, :] = np.eye(128, dtype=np.float32)
    consts[:, 1, :] = np.triu(np.ones((128, 128), np.float32))
    consts[:, 2, :] = 1.0
    consts[:, 3, :] = np.where(np.triu(np.ones((128, 128), bool)), 0.0, NEG)
    cq = f(inputs["conv_q_w"])[0]
    ck = f(inputs["conv_k_w"])[0]
    convw = np.stack([cq, ck], 0).reshape(2, 4, KC, 128).transpose(3, 0, 2, 1)
    bif = np.concatenate([f(inputs["b_i"])[0], f(inputs["b_f"])[0]])
    bif = np.broadcast_to(bif.reshape(1, 1, 8), (128, NT, 8))
    lam = np.stack([f(inputs["lambda_q1"])[0], f(inputs["lambda_k1"])[0], f(inputs["lambda_q2"])[0], f(inputs["lambda_k2"])[0]], 0)
    lam = np.broadcast_to(lam.reshape(1, 4, 64), (128, 4, 64))
    rb = f(inputs["rel_bias"])
    kk = np.arange(128)[:, None]
    qq = np.arange(128)[None, :]
    idx0 = _t5_bucket(qq - kk)
    idx1 = _t5_bucket(128 + qq - kk)
    relb = np.stack([rb[idx0], rb[idx1]], 0)
    relb = relb.transpose(3, 1, 0, 2)
    c31 = np.broadcast_to(rb[31].reshape(1, 8), (128, 8))
    common = {
        "consts": consts,
        "normw": rep(inputs["norm_w"]),
        "fnw": rep(inputs["final_norm_w"]),
        "mnw": rep(inputs["m_norm_w"]),
        "anw": rep(inputs["a_norm_w"]),
        "convw": np.ascontiguousarray(convw),
        "bif": np.ascontiguousarray(bif),
        "lam": np.ascontiguousarray(lam),
        "relb": np.ascontiguousarray(relb),
        "c31": np.ascontiguousarray(c31),
        "bada": rep(inputs["b_ada"]),
        "w_ada": f(inputs["w_ada"])[0],
        "w_in": f(inputs["w_in"])[0],
        "w_out": f(inputs["w_out"])[0],
    }
    in_maps = []
    for i in range(N_CORES):
        m = dict(common)
        m["x"] = np.ascontiguousarray(x[i * NB:(i + 1) * NB])
        cs = c[i * NB:(i + 1) * NB]
        m["cT"] = np.ascontiguousarray(cs.reshape(NB, KC, 128).transpose(2, 1, 0))
        in_maps.append(m)
    return in_maps


def kernel(**inputs):
    in_maps = prep_inputs(inputs)
    nc = build_program()
    res = run_bass_kernel_spmd(nc, in_maps, core_ids=list(range(N_CORES)))
    out = np.concatenate([np.asarray(r["out"]) for r in res.results], axis=0)
    return out.astype(np.float32)
```

```python
import math
from contextlib import ExitStack

import numpy as np
import concourse.bass as bass
import concourse.mybir as mybir
from concourse.bass_utils import run_bass_kernel_spmd

F32 = mybir.dt.float32
BF16 = mybir.dt.bfloat16
ALU = mybir.AluOpType
AF = mybir.ActivationFunctionType

N_CORES = 8
D = 1024
S_LEN = 2048
NB = 2
NT = S_LEN // 128
KC = D // 128
MQ, MK, MV, MO, MZ, MI, AQ, AK, AV, AZ = 0, 1024, 2048, 3072, 4096, 5120, 5128, 6152, 7176, 8200
D_IN = 9224
EPS = 1e-6
WA = 260
WV = 132
NEG = -30000.0

COMPUTE = ("pe", "act", "dve", "pool")
ALLENG = COMPUTE + ("sp",)


class Op:
    __slots__ = ("eng", "fn", "deps", "signal", "seq", "dma", "sem", "val", "pv")

    def __init__(self, eng, fn, deps, dma):
        self.eng = eng
        self.fn = fn
        self.deps = deps
        self.signal = False
        self.seq = 0
        self.dma = dma
        self.sem = None
        self.val = 0
        self.pv = 0


class Sched:
    def __init__(self, nc, n_dma_sems=32):
        self.nc = nc
        self.ops = []
        self.tw = {}
        self.tr = {}
        self.eseq = {e: 0 for e in ALLENG}
        self.n_dma_sems = n_dma_sems

    SUBS = {}
    PH = "PHASE"

    def _expand(self, toks, ph):
        out = []
        for t in toks:
            out.append(t)
            if isinstance(t, tuple) and len(t) == 2 and t[0] == "ps" and t[1] in self.SUBS:
                out.extend(self.SUBS[t[1]])
        if ph and self.PH not in out:
            out.append(self.PH)
        return out

    cut = None
    marks = {}

    def mark(self, name):
        if name not in self.marks:
            self.marks[name] = len(self.ops)

    def op(self, eng, fn, reads=(), writes=(), dma=False, ph=True):
        idx = len(self.ops)
        if self.cut is not None and idx >= self.cut:
            return -1
        deps = set()
        tw, tr = self.tw, self.tr
        writes = self._expand(writes, False)
        reads = self._expand(reads, ph and (self.PH not in writes))
        for t in reads:
            w = tw.get(t)
            if w is not None:
                deps.add(w)
        for t in writes:
            w = tw.get(t)
            if w is not None:
                deps.add(w)
            r = tr.get(t)
            if r:
                deps.update(r)
        for t in reads:
            tr.setdefault(t, []).append(idx)
        for t in writes:
            tw[t] = idx
            tr[t] = []
        o = Op(eng, fn, deps, dma)
        self.eseq[eng] += 1
        o.seq = self.eseq[eng]
        self.ops.append(o)
        return idx

    def dma(self, out, in_, reads=(), writes=(), q="sp", ph=True):
        return self.op(q, lambda e, o=out, i=in_: e.dma_start(out=o, in_=i), reads, writes, dma=True, ph=ph)

    def emit(self, stack):
        nc = self.nc
        ops = self.ops
        for o in ops:
            keep = set()
            best = {}
            for d in o.deps:
                p = ops[d]
                if p.dma:
                    keep.add(d)
                    continue
                if p.eng == o.eng:
                    if o.eng == "pe" and not o.dma:
                        continue
                    if o.seq - p.seq > 2:
                        continue
                if p.eng not in best or best[p.eng][0] < p.seq:
                    best[p.eng] = (p.seq, d)
            for _e, (_sq, d) in best.items():
                ops[d].signal = True
                keep.add(d)
            o.deps = keep
        sems = {e: stack.enter_context(nc.semaphore("s_" + e)) for e in COMPUTE}
        dpools = {"sp": [stack.enter_context(nc.semaphore("dh%d" % i)) for i in range(self.n_dma_sems)],
                  "pool": [stack.enter_context(nc.semaphore("ds%d" % i)) for i in range(8)]}
        cnt = {e: 0 for e in COMPUTE}
        dcount = {}
        dk = {"sp": 0, "pool": 0}
        for o in ops:
            if o.dma:
                pool_ = dpools[o.eng]
                o.sem = pool_[dk[o.eng] % len(pool_)]
                dk[o.eng] += 1
                o.pv = dcount.get(id(o.sem), (None, 0))[1]
                o.val = o.pv + 16
                dcount[id(o.sem)] = (o.sem, o.val)
                o.signal = True
            else:
                if o.signal:
                    cnt[o.eng] += 1
                o.sem = sems[o.eng]
                o.val = cnt[o.eng]
        per = {e: [] for e in ALLENG}
        for o in ops:
            per[o.eng].append(o)
        block = stack.enter_context(nc.Block())

        def run(eng_name, engine):
            waited = {}
            for o in per[eng_name]:
                need = {}
                for d in o.deps:
                    p = ops[d]
                    key = id(p.sem)
                    if need.get(key, (None, 0))[1] < p.val:
                        need[key] = (p.sem, p.val)
                if o.dma and o.pv > 0:
                    key = id(o.sem)
                    if need.get(key, (None, 0))[1] < o.pv:
                        need[key] = (o.sem, o.pv)
                for key, (s, v) in need.items():
                    if waited.get(key, 0) < v:
                        engine.wait_ge(s, v)
                        waited[key] = v
                ins = o.fn(engine)
                if o.signal:
                    ins.then_inc(o.sem, 16 if o.dma else 1)
            if eng_name == "sp":
                for key, (sm_, v) in dcount.items():
                    if waited.get(key, 0) < v:
                        engine.wait_ge(sm_, v)

        @block.tensor
        def _(e):
            run("pe", e)

        @block.scalar
        def _(e):
            run("act", e)

        @block.vector
        def _(e):
            run("dve", e)

        @block.gpsimd
        def _(e):
            run("pool", e)

        @block.sync
        def _(e):
            run("sp", e)


class Mem:
    def __init__(self, nc, limit=224 * 1024 - 256):
        self.nc = nc
        self.off = 16 * 1024 + 256
        self.limit = limit
        self.n = 0
        self.work = None
        self.work_base = 0

    def alloc(self, name, shape, dt, at=None):
        esz = 2 if dt == BF16 else 4
        nel = int(np.prod(shape[1:]))
        nbytes = (nel * esz + 63) // 64 * 64
        if at is None:
            at = self.off
            self.off += nbytes
            assert self.off <= self.limit, (name, self.off)
            self.n += 1
            t = self.nc.alloc_sbuf_tensor_at("%s_%d" % (name, self.n), list(shape), dt, offset=at)
            return t, at + nbytes
        if self.work is None:
            self.work_base = self.off
            nwords = (self.limit - self.off) // 4
            self.work = self.nc.alloc_sbuf_tensor_at("work", [128, nwords], F32, offset=self.off)
        assert at + nbytes <= self.limit, (name, at + nbytes)
        w0 = (at - self.work_base) // 4
        v = self.work[:, w0:w0 + (nel * esz + 3) // 4]
        if dt == BF16:
            v = v.bitcast(BF16)[:, 0:nel]
        if len(shape) == 3:
            v = v.rearrange("p (a b) -> p a b", a=shape[1])
        elif len(shape) == 4:
            v = v.rearrange("p (a b c) -> p a b c", a=shape[1], b=shape[2])
        return v, at + nbytes


def build_program(stage=99, dumps=None, SUB=0, MUT=0):
    nc = bass.Bass("TRN2", target_bir_lowering=False)
    dr = lambda name, shape, dt=F32: nc.dram_tensor(name, list(shape), dt, kind="ExternalInput").ap()
    x_d = dr("x", [NB, S_LEN, D])
    cT_d = dr("cT", [128, KC, NB])
    consts_d = dr("consts", [128, 4, 128])
    normw_d = dr("normw", [128, D])
    fnw_d = dr("fnw", [128, D])
    mnw_d = dr("mnw", [128, D])
    anw_d = dr("anw", [128, 128])
    convw_d = dr("convw", [128, 2, KC, 4])
    bif_d = dr("bif", [128, NT, 8])
    lam_d = dr("lam", [128, 4, 64])
    relb_d = dr("relb", [8, 128, 2, 128])
    c31_d = dr("c31", [128, 8])
    bada_d = dr("bada", [128, 3 * D])
    wada_d = dr("w_ada", [D, 3 * D]).rearrange("(k p) j -> p k j", p=128)
    win_d = dr("w_in", [D, D_IN]).rearrange("(k p) j -> p k j", p=128)
    wout_d = dr("w_out", [2 * D, D]).rearrange("(k p) j -> p k j", p=128)
    out_d = nc.dram_tensor("out", [NB, S_LEN, D], F32, kind="ExternalOutput").ap()

    st = ExitStack()
    S = Sched(nc)
    M = Mem(nc)
    A = lambda name, shape, dt=F32: M.alloc(name, shape, dt)[0]

    def dump(name, ap, shape, reads, dt=F32):
        if dumps is None:
            return
        d = nc.dram_tensor("dbg_" + name, list(shape), dt, kind="ExternalOutput").ap()
        dumps[name] = (tuple(shape), dt)
        S.dma(d, ap, reads=reads)

    cst = A("cst", [128, 4, 128])
    identb = A("identb", [128, 128], BF16)
    mhalf = A("mhalf", [128, 16])
    zeros_f = A("zeros_f", [128, 16])
    normw = A("normw", [128, D])
    gt = A("gt", [128, D])
    sht = A("sht", [128, D])
    gatet = A("gatet", [128, D])
    fnw = A("fnw", [128, D])
    mnw = A("mnw", [128, D])
    anw = A("anw", [128, 128])
    convw = A("convw", [128, 2, KC, 4])
    bif = A("bif", [128, NT, 8])
    lamt = A("lamt", [128, 4, 64])
    lams = A("lams", [128, 16])
    c31 = A("c31", [128, 8])
    cT = A("cT", [128, KC, NB])
    cth = A("cth", [128, KC, NB])
    sc2 = A("sc2", [128, KC, NB])
    scB = A("scB", [128, KC, 128])
    wada = A("wada", [128, KC, 256])
    badap = A("badap", [128, 256])
    wg = A("wg", [128, KC, 8], BF16)
    gpre = A("gpre", [128, NT, 8])
    gef = A("gef", [128, NT, 4])
    glf = A("glf", [128, NT, 4])
    gtmp = A("gtmp", [128, NT, 4])
    g_u = A("g_u", [128, NT, 4])
    g_ebs = A("g_ebs", [128, NT, 4])
    g_ebL = A("g_ebL", [128, NT, 4])
    ssq = A("ssq", [128, NT])
    rstd = A("rstd", [128, NT])
    hT = A("hT", [128, KC, S_LEN], BF16)
    hcatT = A("hcatT", [128, 2 * KC, S_LEN], BF16)
    NSLOT = 8
    slots = [A("slot%d" % i, [128, 2048], BF16) for i in range(NSLOT)]
    lprod = A("lprod", [128, 2, 64])
    base = M.off

    ps = [st.enter_context(nc.psum_tensor("ps%d" % i, [128, 512], F32)) for i in range(8)]
    psb = [p[:].bitcast(BF16) for p in ps]

    ident_f = cst[:, 0, :]
    triu_f = cst[:, 1, :]
    ones_f = cst[:, 2, :]
    maskneg = cst[:, 3, :]

    S.dma(cst[:], consts_d, writes=["cst"])
    S.op("dve", lambda e: e.tensor_copy(out=identb[:], in_=ident_f), ["cst"], ["identb"])
    S.op("pool", lambda e: e.memset(mhalf[:], -0.5), [], ["mhalf"])
    S.op("pool", lambda e: e.memset(zeros_f[:], 0.0), [], ["zeros"])
    for t_sb, t_d, nm in ((normw, normw_d, "normw"), (fnw, fnw_d, "fnw"), (mnw, mnw_d, "mnw"), (anw, anw_d, "anw"),
                          (convw, convw_d, "convw"), (bif, bif_d, "bif"), (lamt, lam_d, "lamt"), (c31, c31_d, "c31"),
                          (cT, cT_d, "cT")):
        S.dma(t_sb[:], t_d, writes=[nm])
    S.dma(wg[:], win_d[:, :, MI:MI + 8], writes=["wg"], q="pool")
    S.op("act", lambda e: e.activation(out=cth[:], in_=cT[:], func=AF.Tanh, scale=0.5), ["cT"], ["cth"])
    S.op("dve", lambda e: e.scalar_tensor_tensor(out=sc2[:], in0=cth[:], scalar=1.0, in1=cT[:], op0=ALU.add, op1=ALU.mult),
         ["cth", "cT"], ["sc2"])
    lam_init = 0.8 - 0.6 * math.exp(-0.3 * 0)
    S.op("dve", lambda e: e.tensor_tensor(out=lprod[:, 0, :], in0=lamt[:, 0, :], in1=lamt[:, 1, :], op=ALU.mult), ["lamt"], ["lprod0"])
    S.op("dve", lambda e: e.tensor_tensor(out=lprod[:, 1, :], in0=lamt[:, 2, :], in1=lamt[:, 3, :], op=ALU.mult), ["lamt"], ["lprod1"])
    S.op("dve", lambda e: e.reduce_sum(out=lams[:, 0:1], in_=lprod[:, 0, :], axis=mybir.AxisListType.X), ["lprod0"], ["lams0"])
    S.op("dve", lambda e: e.reduce_sum(out=lams[:, 1:2], in_=lprod[:, 1, :], axis=mybir.AxisListType.X), ["lprod1"], ["lams1"])
    S.op("act", lambda e: e.activation(out=lams[:, 2:4], in_=lams[:, 0:2], func=AF.Exp), ["lams0", "lams1"], ["lams23"])
    S.op("dve", lambda e: e.tensor_tensor(out=lams[:, 4:5], in0=lams[:, 2:3], in1=lams[:, 3:4], op=ALU.subtract), ["lams23"], ["lams4"])
    S.op("dve", lambda e: e.tensor_scalar(out=lams[:, 5:6], in0=lams[:, 4:5], scalar1=lam_init, scalar2=-1.0, op0=ALU.add, op1=ALU.mult),
         ["lams4"], ["lams5"])
    neg_lam = lams[:, 5:6]
    dump("neglam", lams[:, 0:6], [128, 6], ["lams5"])

    ada_bank = [4]

    def adaln(b, which):
        S.op("dve", lambda e: e.tensor_copy(out=scB[:], in_=sc2[:, :, b:b + 1].to_broadcast([128, KC, 128])), ["sc2"], ["scB"])
        pieces = range(0, 8) if which == 0 else range(8, 12)
        for p in pieces:
            j0 = p * 256
            S.dma(wada[:], wada_d[:, :, j0:j0 + 256], writes=["wada"])
            S.dma(badap[:], bada_d[:, j0:j0 + 256], writes=["badap"])
            bk = ada_bank[0]
            ada_bank[0] = 4 + (bk - 4 + 1) % 2
            for kc in range(KC):
                S.op("pe", lambda e, kc=kc, bk=bk: e.matmul(ps[bk][:, 0:256], lhsT=scB[:, kc, :], rhs=wada[:, kc, :],
                                                           start=(kc == 0), stop=(kc == KC - 1)),
                     ["scB", "wada"], [("ps", bk)])
            if p < 4:
                dst = sht[:, j0:j0 + 256]
                S.op("dve", lambda e, bk=bk, dst=dst: e.scalar_tensor_tensor(out=dst, in0=ps[bk][:, 0:256], scalar=0.5, in1=badap[:],
                                                                              op0=ALU.mult, op1=ALU.add),
                     [("ps", bk), "badap"], [("sht", p)])
            elif p < 8:
                c0 = j0 - D
                dst = gt[:, c0:c0 + 256]
                S.op("dve", lambda e, bk=bk, dst=dst: e.scalar_tensor_tensor(out=dst, in0=ps[bk][:, 0:256], scalar=0.5, in1=badap[:],
                                                                              op0=ALU.mult, op1=ALU.add),
                     [("ps", bk), "badap"], [("gt", p - 4)])
                S.op("dve", lambda e, dst=dst, c0=c0: e.scalar_tensor_tensor(out=dst, in0=dst, scalar=1.0, in1=normw[:, c0:c0 + 256],
                                                                              op0=ALU.add, op1=ALU.mult),
                     [("gt", p - 4), "normw"], [("gt", p - 4)])
            else:
                c0 = j0 - 2 * D
                dst = gatet[:, c0:c0 + 256]
                S.op("dve", lambda e, bk=bk, dst=dst: e.scalar_tensor_tensor(out=dst, in0=ps[bk][:, 0:256], scalar=0.5, in1=badap[:],
                                                                              op0=ALU.mult, op1=ALU.add),
                     [("ps", bk), "badap"], [("gatet", p - 8)])

    pieces = []
    for _b in range(NB):
        for hd in range(4):
            for c0 in (MQ, MK, MV, MO, MZ):
                pieces.append((win_d[:, :, c0 + hd * 256:c0 + hd * 256 + 256], KC, 256))
        for pr in range(4):
            for c0 in (AQ, AK, AV, AZ):
                pieces.append((win_d[:, :, c0 + pr * 256:c0 + pr * 256 + 256], KC, 256))
        for i in range(8):
            pieces.append((wout_d[:, 2 * i:2 * i + 2, :], 2, 1024))
    PIECES_PER_SEQ = 20 + 16 + 8
    pstate = {"issued": 0}

    def issue_piece():
        i = pstate["issued"]
        if i >= len(pieces):
            return
        pstate["issued"] += 1
        src, k, n = pieces[i]
        dst = slots[i % NSLOT][:, 0:k * n].rearrange("p (k n) -> p k n", k=k)
        S.dma(dst, src, writes=[("slot", i % NSLOT)], q="pool", ph=False)

    def release(i):
        while pstate["issued"] <= i + NSLOT and pstate["issued"] < len(pieces):
            issue_piece()

    def pview(i, k, n):
        return slots[i % NSLOT][:, 0:k * n].rearrange("p (k n) -> p k n", k=k)

    def ptok(i):
        return ("slot", i % NSLOT)

    rot_state = {"n": 0}

    def rot():
        r = rot_state["n"] % 4
        rot_state["n"] += 1
        return r

    PH = "PHASE"

    def barrier():
        S.op("pool", lambda e: e.memset(mhalf[:, 8:16], -0.5), [], [PH])

    S.op("dve", lambda e: e.tensor_scalar(out=mnw[:], in0=mnw[:], scalar1=0.25, scalar2=None, op0=ALU.mult), ["mnw"], ["mnw4"])
    S.op("dve", lambda e: e.tensor_scalar(out=anw[:], in0=anw[:], scalar1=(1.0 - lam_init) * 0.5, scalar2=None, op0=ALU.mult), ["anw"], ["anw2"])
    for _ in range(NSLOT):
        issue_piece()

    def seq_body(b):
        if stage < 1:
            return
        adaln(b, 0)
        barrier()
        off = base
        xt = []
        tmpf = []
        hb = []
        for i in range(2):
            t_, off = M.alloc("xt", [128, D], F32, at=off); xt.append(t_)
            t_, off = M.alloc("tmpf", [128, D], F32, at=off); tmpf.append(t_)
            t_, off = M.alloc("hb", [128, D], BF16, at=off); hb.append(t_)
        for tt in range(NT):
            i = tt % 2
            S.dma(xt[i][:], x_d[b, tt * 128:(tt + 1) * 128, :], reads=[PH], writes=[("xt", i)])
            S.op("act", lambda e, i=i, tt=tt: e.activation(out=tmpf[i][:], in_=xt[i][:], func=AF.Square, scale=1.0 / 32.0,
                                                          accum_out=ssq[:, tt:tt + 1]),
                 [("xt", i), PH], [("tmpf", i), ("ssq", tt)])
            S.op("pool", lambda e, tt=tt: e.tensor_scalar(out=rstd[:, tt:tt + 1], in0=ssq[:, tt:tt + 1], scalar1=EPS, scalar2=1.0,
                                                          op0=ALU.add, op1=ALU.mult), [("ssq", tt)], [("rstd", tt)])
            S.op("pool", lambda e, tt=tt: e.tensor_tensor(out=rstd[:, tt:tt + 1], in0=rstd[:, tt:tt + 1], in1=mhalf[:, 0:1], op=ALU.pow),
                 [("rstd", tt), "mhalf"], [("rstd", tt)])
            S.op("dve", lambda e, i=i, tt=tt: e.scalar_tensor_tensor(out=tmpf[i][:], in0=xt[i][:], scalar=rstd[:, tt:tt + 1], in1=gt[:],
                                                                      op0=ALU.mult, op1=ALU.mult),
                 [("xt", i), ("rstd", tt), PH] + [("gt", q) for q in range(4)], [("tmpf", i)])
            S.op("pool", lambda e, i=i: e.tensor_tensor(out=hb[i][:], in0=tmpf[i][:], in1=sht[:], op=ALU.add),
                 [("tmpf", i), PH] + [("sht", q) for q in range(4)], [("hb", i)])
            bk = 6 + tt % 2
            for kc in range(KC):
                S.op("pe", lambda e, i=i, kc=kc, bk=bk: e.transpose(psb[bk][:, kc * 128:(kc + 1) * 128], hb[i][:, kc * 128:(kc + 1) * 128], identb[:]),
                     [("hb", i), "identb", PH], [("ps", bk)])
            S.op("act", lambda e, tt=tt, bk=bk: e.activation(out=hT[:, :, tt * 128:(tt + 1) * 128],
                                                            in_=psb[bk][:].rearrange("p (k t) -> p k t", k=KC), func=AF.Copy),
                 [("ps", bk)], [("hT", tt)])
        if b == 0:
            dump("gt", gt[:], [128, D], [("gt", q) for q in range(4)])
            dump("sht", sht[:], [128, D], [("sht", q) for q in range(4)])
            dump("hT", hT[:], [128, KC, S_LEN], [("hT", q) for q in range(NT)], BF16)
        if stage < 2:
            return
        hT_all = [("hT", q) for q in range(NT)]
        for tt in range(NT):
            for kc in range(KC):
                S.op("pe", lambda e, tt=tt, kc=kc: e.matmul(ps[2][:, tt * 8:(tt + 1) * 8], lhsT=hT[:, kc, tt * 128:(tt + 1) * 128],
                                                           rhs=wg[:, kc, :], start=(kc == 0), stop=(kc == KC - 1)),
                     [("hT", tt), "wg"], [("ps", 2)])
        S.op("dve", lambda e: e.tensor_tensor(out=gpre[:], in0=ps[2][:, 0:128].rearrange("p (t g) -> p t g", g=8), in1=bif[:], op=ALU.add),
             [("ps", 2), "bif"], ["gpre"])
        S.op("act", lambda e: e.activation(out=gef[:], in_=gpre[:, :, 4:8], func=AF.Exp, scale=-1.0), ["gpre"], ["gef"])
        S.op("act", lambda e: e.activation(out=glf[:], in_=gef[:], func=AF.Ln, bias=1.0), ["gef"], ["glf"])
        glf2 = glf[:].rearrange("p t g -> p (t g)")
        S.op("pe", lambda e: e.matmul(ps[3][:, 0:64], lhsT=triu_f, rhs=glf2, start=True, stop=True), ["glf", "cst"], [("ps", 3)])
        S.op("pe", lambda e: e.matmul(ps[3][:, 64:128], lhsT=ones_f, rhs=glf2, start=True, stop=True), ["glf", "cst"], [("ps", 3)])
        cum = ps[3][:, 0:64].rearrange("p (t g) -> p t g", g=4)
        tot = ps[3][:, 64:128].rearrange("p (t g) -> p t g", g=4)
        S.op("dve", lambda e: e.tensor_tensor(out=gtmp[:], in0=gpre[:, :, 0:4], in1=cum, op=ALU.add), ["gpre", ("ps", 3)], ["gtmp"])
        S.op("act", lambda e: e.activation(out=g_u[:], in_=gtmp[:], func=AF.Exp), ["gtmp"], ["g_u"])
        S.op("act", lambda e: e.activation(out=g_ebs[:], in_=cum, func=AF.Exp, scale=-1.0, bias=math.log(1.0 / 64.0)),
             [("ps", 3)], ["g_ebs"])
        S.op("act", lambda e: e.activation(out=g_ebL[:], in_=tot, func=AF.Exp, scale=-1.0), [("ps", 3)], ["g_ebL"])
        if b == 0:
            dump("g_u", g_u[:], [128, NT, 4], ["g_u"])
            dump("g_ebs", g_ebs[:], [128, NT, 4], ["g_ebs"])
            dump("g_ebL", g_ebL[:], [128, NT, 4], ["g_ebL"])
        if stage < 3:
            return
        pbase = b * PIECES_PER_SEQ
        def mhead(hd):
            barrier()
            off = base
            qT, off = M.alloc("qT", [128, 2, S_LEN], BF16, at=off)
            kT, off = M.alloc("kT", [128, 2, S_LEN], BF16, at=off)
            base2 = off
            qkT = (qT, kT)
            pre = []; acc = []; thb = []
            for i in range(2):
                t_, off = M.alloc("pre", [128, 515], F32, at=off); pre.append(t_)
                t_, off = M.alloc("acc", [128, 512], F32, at=off); acc.append(t_)
                t_, off = M.alloc("thb", [128, 512], F32, at=off); thb.append(t_)
            halo, off = M.alloc("halo", [128, 4, 4], F32, at=off)
            p_q = pbase + hd * 5
            n = 0
            for qk in range(2):
                pi_ = p_q + qk
                wv = pview(pi_, KC, 256)
                for dc in range(2):
                    dcg = hd * 2 + dc
                    for tg in range(4):
                        bk = rot()
                        pi = n % 2
                        n += 1
                        for kc in range(KC):
                            S.op("pe", lambda e, bk=bk, wv=wv, kc=kc, dc=dc, tg=tg: e.matmul(
                                ps[bk][:, 0:512], lhsT=wv[:, kc, dc * 128:(dc + 1) * 128], rhs=hT[:, kc, tg * 512:(tg + 1) * 512],
                                start=(kc == 0), stop=(kc == KC - 1)),
                                [ptok(pi_)] + hT_all[4 * tg:4 * tg + 4], [("ps", bk)])
                        S.op("act", lambda e, bk=bk, pi=pi: e.activation(out=pre[pi][:, 3:515], in_=ps[bk][:, 0:512], func=AF.Copy),
                             [("ps", bk), PH], [("pre", pi)])
                        hl = halo[:, qk * 2 + dc, 0:3]
                        if tg == 0:
                            S.op("pool", lambda e, pi=pi: e.memset(pre[pi][:, 0:3], 0.0), [PH], [("preh", pi)])
                        else:
                            S.op("pool", lambda e, pi=pi, hl=hl: e.tensor_copy(out=pre[pi][:, 0:3], in_=hl), [("halo", qk, dc), PH], [("preh", pi)])
                        S.op("pool", lambda e, pi=pi, hl=hl: e.tensor_copy(out=hl, in_=pre[pi][:, 512:515]), [("pre", pi), PH], [("halo", qk, dc)])
                        cw = lambda j, qk=qk, dcg=dcg: convw[:, qk, dcg, j:j + 1]
                        S.op("dve", lambda e, pi=pi, cw=cw: e.tensor_scalar(out=acc[pi][:], in0=pre[pi][:, 3:515], scalar1=cw(3), scalar2=None, op0=ALU.mult),
                             [("pre", pi), "convw", PH], [("acc", pi)])
                        for j in (2, 1, 0):
                            S.op("dve", lambda e, pi=pi, cw=cw, j=j: e.scalar_tensor_tensor(out=acc[pi][:], in0=pre[pi][:, j:j + 512], scalar=cw(j),
                                                                                         in1=acc[pi][:], op0=ALU.mult, op1=ALU.add),
                                 [("pre", pi), ("preh", pi), "convw", ("acc", pi), PH], [("acc", pi)])
                        S.op("act", lambda e, pi=pi: e.activation(out=thb[pi][:], in_=acc[pi][:], func=AF.Tanh, scale=0.5),
                             [("acc", pi), PH], [("thb", pi)])
                        S.op("pool", lambda e, pi=pi: e.tensor_scalar(out=thb[pi][:], in0=thb[pi][:], scalar1=1.0, scalar2=1.0, op0=ALU.add, op1=ALU.mult),
                             [("thb", pi), PH], [("thb", pi)])
                        dst = qkT[qk][:, dc, tg * 512:(tg + 1) * 512]
                        S.op("pool", lambda e, pi=pi, dst=dst: e.tensor_tensor(out=dst, in0=thb[pi][:], in1=acc[pi][:], op=ALU.mult),
                             [("thb", pi), ("acc", pi), PH], [("qkT", qk, dc, tg)])
                release(pi_)
            if b == 0 and hd == 0:
                dump("m_qT", qT[:], [128, 2, S_LEN], [("qkT", 0, dc, tg) for dc in range(2) for tg in range(4)], BF16)
                dump("m_kT", kT[:], [128, 2, S_LEN], [("qkT", 1, dc, tg) for dc in range(2) for tg in range(4)], BF16)
            if SUB == 1:
                return
            barrier()
            off = base2
            uv = []; Gp = []; kTok = []; maskS = []; Hn = []; hmb = []; sm = []; stats = []; mv = []
            for i in range(2):
                t_, off = M.alloc("uv", [128, WA], BF16, at=off); uv.append(t_)
                t_, off = M.alloc("Gp", [128, 256], F32, at=off); Gp.append(t_)
                t_, off = M.alloc("kTok", [128, 256], BF16, at=off); kTok.append(t_)
                t_, off = M.alloc("maskS", [128, 128], BF16, at=off); maskS.append(t_)
                t_, off = M.alloc("Hn", [128, 256], F32, at=off); Hn.append(t_)
                t_, off = M.alloc("hmb", [128, 256], BF16, at=off); hmb.append(t_)
                t_, off = M.alloc("sm", [128, 16], F32, at=off); sm.append(t_)
                t_, off = M.alloc("stats", [128, 6], F32, at=off); stats.append(t_)
                t_, off = M.alloc("mv", [128, 2], F32, at=off); mv.append(t_)
            to_, off = M.alloc("to", [128, 256], F32, at=off)
            tz_, off = M.alloc("tz", [128, 256], F32, at=off)
            zs_, off = M.alloc("zs", [128, 256], F32, at=off)
            Sf, off = M.alloc("Sf", [128, 128], F32, at=off)
            dCs, off = M.alloc("dCs", [128, 2, WA], F32, at=off)
            Mst, off = M.alloc("Mst", [128, 2, WA], F32, at=off)
            Cb, off = M.alloc("Cb", [128, 2, WA], BF16, at=off)
            assert off <= M.limit, off
            p_v, p_o, p_z = p_q + 2, p_q + 3, p_q + 4
            wvv, wvo, wvz = pview(p_v, KC, 256), pview(p_o, KC, 256), pview(p_z, KC, 256)
            mn4 = mnw[:, hd * 256:(hd + 1) * 256]
            for i in range(2):
                S.op("dve", lambda e, i=i: e.tensor_copy(out=uv[i][:, 256:WA], in_=zeros_f[:, 0:WA - 256]), ["zeros"], [("uv1", i)])
            def front(c):
                ci = c % 2
                tsl = slice(c * 128, (c + 1) * 128)
                bA, bB = rot(), rot()
                for (wv_, pt_, bk, c0) in ((wvv, p_v, bA, 0), (wvo, p_o, bA, 256), (wvz, p_z, bB, 0)):
                    for kc in range(KC):
                        S.op("pe", lambda e, wv_=wv_, bk=bk, c0=c0, kc=kc, tsl=tsl: e.matmul(
                            ps[bk][:, c0:c0 + 256], lhsT=hT[:, kc, tsl], rhs=wv_[:, kc, :], start=(kc == 0), stop=(kc == KC - 1)),
                            [ptok(pt_), ("hT", c)], [("ps", bk)])
                u_c = g_u[:, c, hd:hd + 1]
                S.op("act", lambda e, ci=ci, bA=bA, u_c=u_c: e.activation(out=uv[ci][:, 0:256], in_=ps[bA][:, 0:256], func=AF.Copy, scale=u_c),
                     [("ps", bA), "g_u", PH], [("uv", ci)])
                S.op("act", lambda e, ci=ci, u_c=u_c: e.activation(out=uv[ci][:, 256:257], in_=u_c, func=AF.Copy), ["g_u", PH], [("uv1", ci)])
                S.op("act", lambda e, bA=bA: e.activation(out=to_[:], in_=ps[bA][:, 256:512], func=AF.Tanh, scale=0.5), [("ps", bA), PH], ["to"])
                S.op("act", lambda e, bB=bB: e.activation(out=tz_[:], in_=ps[bB][:, 0:256], func=AF.Tanh, scale=0.5), [("ps", bB), PH], ["tz"])
                S.op("act", lambda e, bB=bB: e.activation(out=zs_[:], in_=ps[bB][:, 0:256], func=AF.Copy), [("ps", bB), PH], ["zs"])
                S.op("dve", lambda e, ci=ci: e.scalar_tensor_tensor(out=Gp[ci][:], in0=to_[:], scalar=1.0, in1=zs_[:], op0=ALU.add, op1=ALU.mult),
                     ["to", "zs", PH], [("Gp", ci)])
                S.op("dve", lambda e, ci=ci: e.scalar_tensor_tensor(out=Gp[ci][:], in0=tz_[:], scalar=1.0, in1=Gp[ci][:], op0=ALU.add, op1=ALU.mult),
                     ["tz", ("Gp", ci), PH], [("Gp", ci)])
                S.op("pool", lambda e, ci=ci: e.tensor_tensor(out=Gp[ci][:], in0=Gp[ci][:], in1=mn4, op=ALU.mult), [("Gp", ci), "mnw4", PH], [("Gp", ci)])
                S.mark("A%d" % c)
                for dc in range(2):
                    S.op("pe", lambda e, dc=dc, tsl=tsl: e.transpose(psb[4][:, dc * 128:(dc + 1) * 128], kT[:, dc, tsl], identb[:]),
                         [("qkT", 1, dc, c // 4), "identb"], [("ps", 4)])
                S.op("act", lambda e, ci=ci: e.activation(out=kTok[ci][:], in_=psb[4][:, 0:256], func=AF.Copy), [("ps", 4), PH], [("kTok", ci)])
                for dc in range(2):
                    S.op("pe", lambda e, dc=dc, tsl=tsl: e.matmul(ps[5][:, 0:128], lhsT=kT[:, dc, tsl], rhs=qT[:, dc, tsl], start=(dc == 0), stop=(dc == 1)),
                         [("qkT", 1, dc, c // 4), ("qkT", 0, dc, c // 4)], [("ps", 5)])
                S.op("act", lambda e: e.activation(out=Sf[:], in_=ps[5][:, 0:128], func=AF.Copy), [("ps", 5), PH], ["Sf"])
                S.op("pool", lambda e, ci=ci: e.tensor_tensor(out=maskS[ci][:], in0=Sf[:], in1=triu_f, op=ALU.mult),
                     ["Sf", "cst", PH], [("maskS", ci)])
            def back(c):
                ci = c % 2
                tsl = slice(c * 128, (c + 1) * 128)
                S.mark("B%d" % c)
                if c > 0:
                    for dc in range(2):
                        S.op("pe", lambda e, dc=dc, tsl=tsl: e.matmul(ps[6][:, 0:WA], lhsT=qT[:, dc, tsl], rhs=Cb[:, dc, :], start=(dc == 0), stop=False),
                             [("qkT", 0, dc, c // 4), "Cb"], [("ps", 6)])
                S.op("pe", lambda e, ci=ci, c=c: e.matmul(ps[6][:, 0:WA], lhsT=maskS[ci][:], rhs=uv[ci][:], start=(c == 0), stop=True),
                     [("maskS", ci), ("uv", ci), ("uv1", ci)], [("ps", 6)])
                ebs_c = g_ebs[:, c, hd:hd + 1]
                smc = sm[ci]
                S.op("act", lambda e, smc=smc: e.activation(out=smc[:, 7:8], in_=ps[6][:, 256:257], func=AF.Copy), [("ps", 6), PH], [("sm", ci, 7)])
                S.op("dve", lambda e, smc=smc, ebs_c=ebs_c: e.tensor_tensor(out=smc[:, 0:1], in0=smc[:, 7:8], in1=ebs_c, op=ALU.mult),
                     [("sm", ci, 7), "g_ebs", PH], [("sm", ci, 0)])
                S.op("dve", lambda e, smc=smc: e.tensor_tensor(out=smc[:, 1:2], in0=smc[:, 0:1], in1=smc[:, 0:1], op=ALU.mult), [("sm", ci, 0), PH], [("sm", ci, 1)])
                S.op("dve", lambda e, smc=smc: e.tensor_scalar(out=smc[:, 2:3], in0=smc[:, 1:2], scalar1=1.0, scalar2=None, op0=ALU.max), [("sm", ci, 1), PH], [("sm", ci, 2)])
                S.op("pool", lambda e, smc=smc: e.tensor_tensor(out=smc[:, 3:4], in0=smc[:, 2:3], in1=mhalf[:, 0:1], op=ALU.pow), [("sm", ci, 2), "mhalf", PH], [("sm", ci, 3)])
                S.op("dve", lambda e, smc=smc, ebs_c=ebs_c: e.tensor_tensor(out=smc[:, 4:5], in0=smc[:, 3:4], in1=ebs_c, op=ALU.mult), [("sm", ci, 3), "g_ebs", PH], [("sm", ci, 4)])
                S.op("act", lambda e, ci=ci, smc=smc: e.activation(out=Hn[ci][:], in_=ps[6][:, 0:256], func=AF.Copy, scale=smc[:, 4:5]),
                     [("ps", 6), ("sm", ci, 4), PH], [("Hn", ci)])
                S.mark("C%d" % c)
                S.op("dve", lambda e, ci=ci: e.bn_stats(out=stats[ci][:], in_=Hn[ci][:]), [("Hn", ci), PH], [("stats", ci)])
                S.op("dve", lambda e, ci=ci: e.bn_aggr(out=mv[ci][:], in_=stats[ci][:]), [("stats", ci), PH], [("mv", ci)])
                S.op("pool", lambda e, ci=ci, smc=smc: e.tensor_scalar(out=smc[:, 5:6], in0=mv[ci][:, 1:2], scalar1=EPS, scalar2=1.0, op0=ALU.add, op1=ALU.mult),
                     [("mv", ci), PH], [("sm", ci, 5)])
                S.op("pool", lambda e, smc=smc: e.tensor_tensor(out=smc[:, 6:7], in0=smc[:, 5:6], in1=mhalf[:, 0:1], op=ALU.pow), [("sm", ci, 5), "mhalf", PH], [("sm", ci, 6)])
                S.op("dve", lambda e, ci=ci, smc=smc: e.tensor_scalar(out=Hn[ci][:], in0=Hn[ci][:], scalar1=mv[ci][:, 0:1], scalar2=smc[:, 6:7],
                                                                      op0=ALU.subtract, op1=ALU.mult),
                     [("Hn", ci), ("mv", ci), ("sm", ci, 6), PH], [("Hn", ci)])
                S.op("pool", lambda e, ci=ci: e.tensor_tensor(out=hmb[ci][:], in0=Hn[ci][:], in1=Gp[ci][:], op=ALU.mult), [("Hn", ci), ("Gp", ci), PH], [("hmb", ci)])
                bT = rot()
                for dc in range(2):
                    S.op("pe", lambda e, ci=ci, dc=dc, bT=bT: e.transpose(psb[bT][:, dc * 128:(dc + 1) * 128], hmb[ci][:, dc * 128:(dc + 1) * 128], identb[:]),
                         [("hmb", ci), "identb"], [("ps", bT)])
                S.op("act", lambda e, tsl=tsl, bT=bT: e.activation(out=hcatT[:, hd * 2:hd * 2 + 2, tsl], in_=psb[bT][:, 0:256].rearrange("p (k t) -> p k t", k=2), func=AF.Copy),
                     [("ps", bT)], [("hcatT", hd * 2, c), ("hcatT", hd * 2 + 1, c)])
                S.mark("D%d" % c)
                if c < NT - 1:
                    S.op("pe", lambda e, ci=ci: e.matmul(ps[7][:, 0:WA], lhsT=kTok[ci][:, 0:128], rhs=uv[ci][:], start=True, stop=True),
                         [("kTok", ci), ("uv", ci), ("uv1", ci)], [("ps", 7)])
                    bD = rot()
                    S.op("pe", lambda e, ci=ci, bD=bD: e.matmul(ps[bD][:, 0:WA], lhsT=kTok[ci][:, 128:256], rhs=uv[ci][:], start=True, stop=True),
                         [("kTok", ci), ("uv", ci), ("uv1", ci)], [("ps", bD)])
                    for dc, (src, tk) in enumerate(((ps[7][:, 0:WA], ("ps", 7)), (ps[bD][:, 0:WA], ("ps", bD)))):
                        S.op("act", lambda e, dc=dc, src=src: e.activation(out=dCs[:, dc, :], in_=src, func=AF.Copy), [tk, PH], [("dCs", dc)])
                        if c == 0:
                            S.op("dve", lambda e, dc=dc: e.tensor_copy(out=Mst[:, dc, :], in_=dCs[:, dc, :]), [("dCs", dc), PH, "Cb"], [("Mst", dc)])
                        else:
                            ebp = g_ebL[:, c - 1, hd:hd + 1]
                            S.op("dve", lambda e, dc=dc, ebp=ebp: e.scalar_tensor_tensor(out=Mst[:, dc, :], in0=Mst[:, dc, :], scalar=ebp, in1=dCs[:, dc, :],
                                                                                         op0=ALU.mult, op1=ALU.add),
                                 [("dCs", dc), ("Mst", dc), "g_ebL", PH], [("Mst", dc)])
                    ebc = g_ebL[:, c, hd:hd + 1]
                    S.op("act", lambda e, ebc=ebc: e.activation(out=Cb[:], in_=Mst[:], func=AF.Copy, scale=ebc), [("Mst", 0), ("Mst", 1), "g_ebL", PH], ["Cb"])
                S.mark("E%d" % c)
            front(0)
            for c in range(NT):
                if c + 1 < NT:
                    front(c + 1)
                back(c)
            release(p_v); release(p_o); release(p_z)

        for hd in range(4 if stage >= 4 else 1):
            mhead(hd)
        if b == 0 and SUB != 1:
            nhd = 8 if stage >= 4 else 2
            dump("hcat_m", hcatT[:, 0:nhd, :], [128, nhd, S_LEN], [("hcatT", k, c) for k in range(nhd) for c in range(NT)], BF16)
        if stage < 5:
            return
        def apair(pr):
            p_aq = pbase + 20 + pr * 4
            p_ak, p_av, p_az = p_aq + 1, p_aq + 2, p_aq + 3
            def ahead(hh):
                ah = pr * 2 + hh
                barrier()
                off = base
                qTa, off = M.alloc("qTa", [128, S_LEN], BF16, at=off)
                kTa, off = M.alloc("kTa", [128, S_LEN], BF16, at=off)
                Vaug, off = M.alloc("Vaug", [128, NT, WV], BF16, at=off)
                Gz = []; Pb = []; tmpS = []; tza = []; t1 = []; t0 = []; zsa = []; ha = []; hab = []; sma = []
                for i in range(2):
                    t_, off = M.alloc("Gz", [128, 4, 128], F32, at=off); Gz.append(t_)
                    t_, off = M.alloc("tmpS", [128, 256], F32, at=off); tmpS.append(t_)
                    t_, off = M.alloc("tza", [128, 128], F32, at=off); tza.append(t_)
                    t_, off = M.alloc("t1", [128, 128], F32, at=off); t1.append(t_)
                    t_, off = M.alloc("t0", [128, 128], F32, at=off); t0.append(t_)
                    t_, off = M.alloc("zsa", [128, 128], F32, at=off); zsa.append(t_)
                    t_, off = M.alloc("ha", [128, 128], F32, at=off); ha.append(t_)
                    t_, off = M.alloc("hab", [128, 128], BF16, at=off); hab.append(t_)
                    t_, off = M.alloc("sma", [128, 16], F32, at=off); sma.append(t_)
                for i in range(3):
                    t_, off = M.alloc("Pb", [128, 512], BF16, at=off); Pb.append(t_)
                relb, off = M.alloc("relb", [128, 2, 128], F32, at=off)
                junk, off = M.alloc("junk", [128, 128], F32, at=off)
                assert off <= M.limit, off
                csl = slice(hh * 128, (hh + 1) * 128)
                wq, wk, wv_, wz = pview(p_aq, KC, 256), pview(p_ak, KC, 256), pview(p_av, KC, 256), pview(p_az, KC, 256)
                S.dma(relb[:], relb_d[ah], writes=["relb"])
                S.op("dve", lambda e: e.tensor_tensor(out=relb[:, 0, :], in0=relb[:, 0, :], in1=maskneg, op=ALU.add), ["relb", "cst"], ["relb"])
                S.op("act", lambda e: e.activation(out=relb[:], in_=relb[:], func=AF.Exp), ["relb"], ["relb"])
                S.op("dve", lambda e: e.tensor_copy(out=Vaug[:, :, 128:WV], in_=ones_f[:, 0:NT * (WV - 128)].rearrange("p (a b) -> p a b", a=NT)), ["cst"], ["Vones"])
                for isk, (w_, pt_, dstT) in enumerate(((wq, p_aq, qTa), (wk, p_ak, kTa))):
                    for tg in range(4):
                        bk = rot()
                        for kc in range(KC):
                            S.op("pe", lambda e, bk=bk, w_=w_, kc=kc, tg=tg: e.matmul(
                                ps[bk][:, 0:512], lhsT=w_[:, kc, csl], rhs=hT[:, kc, tg * 512:(tg + 1) * 512], start=(kc == 0), stop=(kc == KC - 1)),
                                [ptok(pt_)] + hT_all[4 * tg:4 * tg + 4], [("ps", bk)])
                        dst = dstT[:, tg * 512:(tg + 1) * 512]
                        if isk == 0:
                            S.op("act", lambda e, bk=bk, dst=dst: e.activation(out=dst, in_=ps[bk][:, 0:512], func=AF.Copy, scale=0.125),
                                 [("ps", bk)], [("qTa", tg)])
                        else:
                            S.op("act", lambda e, bk=bk, dst=dst: e.activation(out=dst, in_=ps[bk][:, 0:512], func=AF.Copy), [("ps", bk)], [("kTa", tg)])
                c31h = c31[:, ah:ah + 1]
                pst = {"p": 0, "s": 0}
                S.mark("AA")
                for G in range(4):
                    gi = G % 2
                    for tt in range(4 * G, 4 * G + 4):
                        bk = rot()
                        tsl = slice(tt * 128, (tt + 1) * 128)
                        for (w_, pt_, c0) in ((wv_, p_av, 0), (wz, p_az, 128)):
                            for kc in range(KC):
                                S.op("pe", lambda e, bk=bk, w_=w_, kc=kc, c0=c0, tsl=tsl: e.matmul(
                                    ps[bk][:, c0:c0 + 128], lhsT=hT[:, kc, tsl], rhs=w_[:, kc, csl], start=(kc == 0), stop=(kc == KC - 1)),
                                    [ptok(pt_), ("hT", tt)], [("ps", bk)])
                        S.op("act", lambda e, bk=bk, tt=tt: e.activation(out=Vaug[:, tt, 0:128], in_=ps[bk][:, 0:128], func=AF.Copy), [("ps", bk)], [("Vaug", tt)])
                        ti = tt % 2
                        S.op("act", lambda e, bk=bk, ti=ti: e.activation(out=tza[ti][:], in_=ps[bk][:, 128:256], func=AF.Tanh, scale=0.5), [("ps", bk)], [("tza", ti)])
                        gdst = Gz[gi][:, tt % 4, :]
                        S.op("act", lambda e, bk=bk, ti=ti: e.activation(out=zsa[ti][:], in_=ps[bk][:, 128:256], func=AF.Copy), [("ps", bk)], [("zsa", ti)])
                        S.op("dve", lambda e, ti=ti, gdst=gdst: e.scalar_tensor_tensor(out=gdst, in0=tza[ti][:], scalar=1.0, in1=zsa[ti][:], op0=ALU.add, op1=ALU.mult),
                             [("tza", ti), ("zsa", ti)], [("Gz", gi, tt % 4)])
                        S.op("pool", lambda e, gdst=gdst: e.tensor_tensor(out=gdst, in0=gdst, in1=anw[:], op=ALU.mult), [("Gz", gi, tt % 4), "anw2"], [("Gz", gi, tt % 4)])
                    S.mark("AB")
                    steps = [(m, j) for m in range(2) for j in range(4 * G + 4)]

                    def emit_scores(m, j):
                        msl = slice(m * 64, (m + 1) * 64)
                        qlo = max(j, 4 * G)
                        nq = 4 * G + 4 - qlo
                        N = nq * 128
                        bk = rot()
                        S.op("pe", lambda e, bk=bk, msl=msl, j=j, qlo=qlo, N=N: e.matmul(
                            ps[bk][:, 0:N], lhsT=kTa[msl, j * 128:(j + 1) * 128], rhs=qTa[msl, qlo * 128:qlo * 128 + N], start=True, stop=True),
                            [("kTa", j // 4)] + [("qTa", G)], [("ps", bk)])
                        pi = pst["p"] % 3
                        pst["p"] += 1
                        nnear = max(0, min(nq, j + 2 - qlo))
                        if nnear > 0:
                            ty0 = qlo - j
                            si = pst["s"] % 2
                            pst["s"] += 1
                            tsv = tmpS[si]
                            S.op("act", lambda e, bk=bk, tsv=tsv, nnear=nnear: e.activation(out=tsv[:, 0:nnear * 128], in_=ps[bk][:, 0:nnear * 128], func=AF.Exp),
                                 [("ps", bk)], [("tmpS", si)])
                            S.op("dve", lambda e, pi=pi, tsv=tsv, nnear=nnear, ty0=ty0: e.tensor_tensor(
                                out=Pb[pi][:, 0:nnear * 128], in0=tsv[:, 0:nnear * 128],
                                in1=relb[:, ty0:ty0 + nnear, :].rearrange("p a q -> p (a q)"), op=ALU.mult),
                                [("tmpS", si), "relb"], [("P", pi, "n")])
                        if nq > nnear:
                            S.op("act", lambda e, pi=pi, bk=bk, nnear=nnear, N=N: e.activation(out=Pb[pi][:, nnear * 128:N], in_=ps[bk][:, nnear * 128:N],
                                                                                            func=AF.Exp, bias=c31h),
                                 [("ps", bk), "c31"], [("P", pi, "f")])
                        return (pi, qlo, nq)

                    def emit_pv(m, j, info):
                        pi, qlo, nq = info
                        for qi in range(nq):
                            qb = qlo + qi
                            ql = qb - 4 * G
                            abk = 4 + m * 2 + ql // 2
                            col0 = (ql % 2) * 256
                            S.op("pe", lambda e, pi=pi, qi=qi, abk=abk, col0=col0, j=j, qb=qb, ql=ql: e.matmul(
                                ps[abk][:, col0:col0 + WV], lhsT=Pb[pi][:, qi * 128:(qi + 1) * 128], rhs=Vaug[:, j, :],
                                start=(j == 0 and ql % 2 == 0), stop=(j == qb and ql % 2 == 1)),
                                [("P", pi, "n"), ("P", pi, "f"), ("Vaug", j), "Vones"], [("ps", abk)])

                    info = emit_scores(*steps[0])
                    for si_, (m, j) in enumerate(steps):
                        nxt = emit_scores(*steps[si_ + 1]) if si_ + 1 < len(steps) else None
                        emit_pv(m, j, info)
                        info = nxt
                    S.mark("AC")
                    for ql in range(4):
                        qb = 4 * G + ql
                        fi = ql % 2
                        a0 = ps[4 + ql // 2][:, (ql % 2) * 256:(ql % 2) * 256 + 129]
                        a1 = ps[6 + ql // 2][:, (ql % 2) * 256:(ql % 2) * 256 + 129]
                        tk0 = ("ps", 4 + ql // 2)
                        tk1 = ("ps", 6 + ql // 2)
                        sm_ = sma[fi]
                        S.op("act", lambda e, sm_=sm_, a0=a0: e.activation(out=sm_[:, 6:7], in_=a0[:, 128:129], func=AF.Copy), [tk0], [("sma", fi, 6)])
                        S.op("act", lambda e, sm_=sm_, a1=a1: e.activation(out=sm_[:, 7:8], in_=a1[:, 128:129], func=AF.Copy), [tk1], [("sma", fi, 7)])
                        S.op("dve", lambda e, sm_=sm_: e.reciprocal(out=sm_[:, 0:1], in_=sm_[:, 6:7]), [("sma", fi, 6)], [("sma", fi, 0)])
                        S.op("dve", lambda e, sm_=sm_: e.reciprocal(out=sm_[:, 1:2], in_=sm_[:, 7:8]), [("sma", fi, 7)], [("sma", fi, 1)])
                        S.op("dve", lambda e, sm_=sm_: e.tensor_tensor(out=sm_[:, 2:3], in0=sm_[:, 1:2], in1=neg_lam, op=ALU.mult), [("sma", fi, 1), "lams5"], [("sma", fi, 2)])
                        S.op("act", lambda e, fi=fi, sm_=sm_, a1=a1: e.activation(out=t1[fi][:], in_=a1[:, 0:128], func=AF.Copy, scale=sm_[:, 2:3]),
                             [tk1, ("sma", fi, 2)], [("t1", fi)])
                        S.op("act", lambda e, fi=fi, sm_=sm_, a0=a0: e.activation(out=t0[fi][:], in_=a0[:, 0:128], func=AF.Copy, scale=sm_[:, 0:1]),
                             [tk0, ("sma", fi, 0)], [("t0", fi)])
                        S.op("dve", lambda e, fi=fi: e.tensor_tensor(out=ha[fi][:], in0=t0[fi][:], in1=t1[fi][:], op=ALU.add),
                             [("t0", fi), ("t1", fi)], [("ha", fi)])
                        S.op("act", lambda e, fi=fi, sm_=sm_: e.activation(out=junk[:], in_=ha[fi][:], func=AF.Square, scale=128.0 ** -0.5, accum_out=sm_[:, 3:4]),
                             [("ha", fi)], ["junk", ("sma", fi, 3)])
                        S.op("pool", lambda e, sm_=sm_: e.tensor_scalar(out=sm_[:, 4:5], in0=sm_[:, 3:4], scalar1=EPS, scalar2=1.0, op0=ALU.add, op1=ALU.mult),
                             [("sma", fi, 3)], [("sma", fi, 4)])
                        S.op("pool", lambda e, sm_=sm_: e.tensor_tensor(out=sm_[:, 5:6], in0=sm_[:, 4:5], in1=mhalf[:, 0:1], op=ALU.pow), [("sma", fi, 4), "mhalf"], [("sma", fi, 5)])
                        S.op("dve", lambda e, fi=fi, sm_=sm_, ql=ql, gi=gi: e.scalar_tensor_tensor(out=hab[fi][:], in0=ha[fi][:], scalar=sm_[:, 5:6], in1=Gz[gi][:, ql, :],
                                                                                             op0=ALU.mult, op1=ALU.mult),
                             [("ha", fi), ("sma", fi, 5), ("Gz", gi, ql)], [("hab", fi)])
                        bk = rot()
                        S.op("pe", lambda e, fi=fi, bk=bk: e.transpose(psb[bk][:, 0:128], hab[fi][:], identb[:]), [("hab", fi), "identb"], [("ps", bk)])
                        S.op("act", lambda e, bk=bk, qb=qb: e.activation(out=hcatT[:, 8 + ah, qb * 128:(qb + 1) * 128], in_=psb[bk][:, 0:128], func=AF.Copy),
                             [("ps", bk)], [("hcatT", 8 + ah, qb)])
            for hh in range(2):
                ahead(hh)
                S.mark("AD")
            release(p_aq); release(p_ak); release(p_av); release(p_az)

        for pr in range(4 if stage >= 6 else 1):
            apair(pr)
        if b == 0:
            nha = 8 if stage >= 6 else 2
            dump("hcat_a", hcatT[:, 8:8 + nha, :], [128, nha, S_LEN], [("hcatT", k, c) for k in range(8, 8 + nha) for c in range(NT)], BF16)
        if stage < 7:
            return
        adaln(b, 1)
        barrier()
        off = base
        xt5 = []; tmpf5 = []; ot5 = []
        for i in range(2):
            t_, off = M.alloc("xt5", [128, D], F32, at=off); xt5.append(t_)
            t_, off = M.alloc("tmpf5", [128, D], F32, at=off); tmpf5.append(t_)
            t_, off = M.alloc("ot5", [128, D], F32, at=off); ot5.append(t_)
        p_w = pbase + 36
        hc_all = lambda tt: [("hcatT", k, tt) for k in range(16)]
        for tt in range(NT):
            i = tt % 2
            tsl = slice(tt * 128, (tt + 1) * 128)
            S.dma(xt5[i][:], x_d[b, tsl, :], writes=[("xt", i)])
            for half in range(2):
                bk = rot()
                for fc in range(16):
                    wv5 = pview(p_w + fc // 2, 2, 1024)
                    S.op("pe", lambda e, bk=bk, fc=fc, wv5=wv5, half=half, tsl=tsl: e.matmul(
                        ps[bk][:, 0:512], lhsT=hcatT[:, fc, tsl], rhs=wv5[:, fc % 2, half * 512:(half + 1) * 512], start=(fc == 0), stop=(fc == 15)),
                        [ptok(p_w + fc // 2), ("hcatT", fc, tt)], [("ps", bk)])
                S.op("act", lambda e, bk=bk, i=i, half=half: e.activation(out=tmpf5[i][:, half * 512:(half + 1) * 512], in_=ps[bk][:, 0:512], func=AF.Copy),
                     [("ps", bk)], [("tmpf", i, half)])
                S.op("dve", lambda e, i=i, half=half: e.tensor_tensor(out=tmpf5[i][:, half * 512:(half + 1) * 512], in0=tmpf5[i][:, half * 512:(half + 1) * 512],
                                                                      in1=gatet[:, half * 512:(half + 1) * 512], op=ALU.mult),
                     [("tmpf", i, half)] + [("gatet", q) for q in range(4)], [("tmpf", i, half)])
            S.op("pool", lambda e, i=i: e.tensor_tensor(out=tmpf5[i][:], in0=tmpf5[i][:], in1=xt5[i][:], op=ALU.add),
                 [("tmpf", i, 0), ("tmpf", i, 1), ("xt", i)], [("tmpf", i, 0), ("tmpf", i, 1)])
            S.op("act", lambda e, i=i, tt=tt: e.activation(out=ot5[i][:], in_=tmpf5[i][:], func=AF.Square, scale=1.0 / 32.0, accum_out=ssq[:, tt:tt + 1]),
                 [("tmpf", i, 0), ("tmpf", i, 1)], [("ot", i), ("ssq", tt)])
            S.op("pool", lambda e, tt=tt: e.tensor_scalar(out=rstd[:, tt:tt + 1], in0=ssq[:, tt:tt + 1], scalar1=EPS, scalar2=1.0, op0=ALU.add, op1=ALU.mult),
                 [("ssq", tt)], [("rstd", tt)])
            S.op("pool", lambda e, tt=tt: e.tensor_tensor(out=rstd[:, tt:tt + 1], in0=rstd[:, tt:tt + 1], in1=mhalf[:, 0:1], op=ALU.pow),
                 [("rstd", tt), "mhalf"], [("rstd", tt)])
            S.op("dve", lambda e, i=i, tt=tt: e.scalar_tensor_tensor(out=ot5[i][:], in0=tmpf5[i][:], scalar=rstd[:, tt:tt + 1], in1=fnw[:], op0=ALU.mult, op1=ALU.mult),
                 [("tmpf", i, 0), ("tmpf", i, 1), ("rstd", tt), "fnw"], [("ot", i)])
            S.dma(out_d[b, tsl, :], ot5[i][:], reads=[("ot", i)])
        for i in range(8):
            release(p_w + i)
    for b in range(NB):
        seq_body(b)
    S.emit(st)
    st.close()
    return nc


def _t5_bucket(n):
    n = np.maximum(n, 0)
    max_exact = 16
    nf = np.maximum(n, 1).astype(np.float32)
    large = max_exact + (np.log(nf / max_exact) / math.log(128 / max_exact) * (32 - max_exact)).astype(np.int32)
    large = np.minimum(large, 31)
    return np.where(n < max_exact, n, large)


def prep_inputs(inputs):
    f = lambda a: np.ascontiguousarray(np.asarray(a, dtype=np.float32))
    x = f(inputs["x"])
    c = f(inputs["c"])
    rep = lambda v, n=128: np.ascontiguousarray(np.broadcast_to(f(v).reshape(1, -1), (n, f(v).size)))
    consts = np.zeros((128, 4, 128), np.float32)
    consts[:, 0, :] = np.eye(128, dtype=np.float32)
    consts[:, 1, :] = np.triu(np.ones((128, 128), np.float32))
    consts[:, 2, :] = 1.0
    consts[:, 3, :] = np.where(np.triu(np.ones((128, 128), bool)), 0.0, NEG)
    cq = f(inputs["conv_q_w"])[0]
    ck = f(inputs["conv_k_w"])[0]
    convw = np.stack([cq, ck], 0).reshape(2, 4, KC, 128).transpose(3, 0, 2, 1)
    bif = np.concatenate([f(inputs["b_i"])[0], f(inputs["b_f"])[0]])
    bif = np.broadcast_to(bif.reshape(1, 1, 8), (128, NT, 8))
    lam = np.stack([f(inputs["lambda_q1"])[0], f(inputs["lambda_k1"])[0], f(inputs["lambda_q2"])[0], f(inputs["lambda_k2"])[0]], 0)
    lam = np.broadcast_to(lam.reshape(1, 4, 64), (128, 4, 64))
    rb = f(inputs["rel_bias"])
    kk = np.arange(128)[:, None]
    qq = np.arange(128)[None, :]
    idx0 = _t5_bucket(qq - kk)
    idx1 = _t5_bucket(128 + qq - kk)
    relb = np.stack([rb[idx0], rb[idx1]], 0)
    relb = relb.transpose(3, 1, 0, 2)
    c31 = np.broadcast_to(rb[31].reshape(1, 8), (128, 8))
    common = {
        "consts": consts,
        "normw": rep(inputs["norm_w"]),
        "fnw": rep(inputs["final_norm_w"]),
        "mnw": rep(inputs["m_norm_w"]),
        "anw": rep(inputs["a_norm_w"]),
        "convw": np.ascontiguousarray(convw),
        "bif": np.ascontiguousarray(bif),
        "lam": np.ascontiguousarray(lam),
        "relb": np.ascontiguousarray(relb),
        "c31": np.ascontiguousarray(c31),
        "bada": rep(inputs["b_ada"]),
        "w_ada": f(inputs["w_ada"])[0],
        "w_in": f(inputs["w_in"])[0],
        "w_out": f(inputs["w_out"])[0],
    }
    in_maps = []
    for i in range(N_CORES):
        m = dict(common)
        m["x"] = np.ascontiguousarray(x[i * NB:(i + 1) * NB])
        cs = c[i * NB:(i + 1) * NB]
        m["cT"] = np.ascontiguousarray(cs.reshape(NB, KC, 128).transpose(2, 1, 0))
        in_maps.append(m)
    return in_maps


def kernel(**inputs):
    in_maps = prep_inputs(inputs)
    nc = build_program()
    res = run_bass_kernel_spmd(nc, in_maps, core_ids=list(range(N_CORES)))
    out = np.concatenate([np.asarray(r["out"]) for r in res.results], axis=0)
    return out.astype(np.float32)
```
